# Optimizing a Trainium2 kernel written in Bass

```python
import math
import jax, jax.numpy as jnp
from jax import lax
import numpy as np

D_MODEL = 1024
BATCH = 1
SEQ = 16384
DEPTH = 1
DEC_BATCH = 16
DEC_SEQ = 64
PAST_LEN = 2048

CHUNK = 64
N_MEM = 256
EPS = 1e-6
SWA_HEADS = 16
SWA_KV_HEADS = 2
SWA_HEAD_DIM = 64
SWA_GROUP = SWA_HEADS // SWA_KV_HEADS
WINDOW = 128
ROPE_THETA = 10000.0
GDN_HEADS = 8
GDN_DK = 128
GDN_DV = 128
CONV_W = 4
MEM_HEADS = 4
MEM_HEAD_DIM = 256
N_BRANCH = 3
D_FF = 4 * D_MODEL

SWA_Q = SWA_HEADS * SWA_HEAD_DIM
SWA_KV = SWA_KV_HEADS * SWA_HEAD_DIM
GDN_QK = GDN_HEADS * GDN_DK
GDN_V = GDN_HEADS * GDN_DV
CONV_CH = 2 * GDN_QK + GDN_V
MEM_Q = MEM_HEADS * MEM_HEAD_DIM
SPLIT_SIZES = (SWA_Q, SWA_KV, SWA_KV, GDN_QK, GDN_QK, GDN_V, GDN_V, GDN_HEADS, GDN_HEADS, MEM_Q, N_BRANCH * D_MODEL)
D_IN = sum(SPLIT_SIZES)

kernel_name = 'streaming_hybrid_swa_gdn_mem'


def split_cols(x, sizes):
    out = []
    start = 0
    for s in sizes:
        out.append(x[..., start:start + s])
        start += s
    return out


def rmsnorm(x, g):
    xf = x.astype(jnp.float32)
    y = xf * lax.rsqrt(jnp.mean(xf * xf, axis=-1, keepdims=True) + EPS)
    return (y * g.astype(jnp.float32)).astype(x.dtype)


def l2norm(x):
    return x * lax.rsqrt(jnp.sum(x * x, axis=-1, keepdims=True) + EPS)


def rope(x, pos):
    half = x.shape[-1] // 2
    inv = ROPE_THETA ** (-jnp.arange(half, dtype=jnp.float32) / half)
    ang = pos.astype(jnp.float32)[:, None] * inv[None, :]
    cos = jnp.cos(ang)[:, None, :]
    sin = jnp.sin(ang)[:, None, :]
    xf = x.astype(jnp.float32)
    x1, x2 = xf[..., :half], xf[..., half:]
    return jnp.concatenate([x1 * cos - x2 * sin, x2 * cos + x1 * sin], axis=-1).astype(x.dtype)


def sink_attention(qg, kb, vb, sinks, valid):
    B, C, Q = qg.shape[:3]
    s = jnp.einsum('bcqkgd,bcskd->bckgqs', qg, kb).astype(jnp.float32) * (SWA_HEAD_DIM ** -0.5)
    s = jnp.where(valid[None, :, None, None, None, :], s, -jnp.inf)
    sink = sinks.astype(jnp.float32).reshape(1, 1, SWA_KV_HEADS, SWA_GROUP, 1, 1)
    m = jnp.maximum(jnp.max(s, axis=-1, keepdims=True), sink)
    e = jnp.exp(s - m)
    p = e / (jnp.sum(e, axis=-1, keepdims=True) + jnp.exp(sink - m))
    o = jnp.einsum('bckgqs,bcskd->bcqkgd', p.astype(vb.dtype), vb)
    return o.reshape(B, C * Q, SWA_Q)


def swa_attention(q, k, v, k_hist, v_hist, hist_valid, sinks):
    B, T = q.shape[:2]
    qb = min(T, CHUNK)
    nc = T // qb
    k_all = jnp.concatenate([k_hist, k], axis=1)
    v_all = jnp.concatenate([v_hist, v], axis=1)
    band = jnp.arange(nc)[:, None] * qb + jnp.arange(WINDOW + qb)[None, :]
    valid_all = jnp.concatenate([jnp.full((WINDOW,), hist_valid, dtype=bool), jnp.ones((T,), dtype=bool)])
    valid = valid_all[band]
    kb = jnp.take(k_all, band, axis=1)
    vb = jnp.take(v_all, band, axis=1)
    qg = q.reshape(B, nc, qb, SWA_KV_HEADS, SWA_GROUP, SWA_HEAD_DIM)
    o = sink_attention(qg, kb, vb, sinks, valid)
    return o, k_all[:, -WINDOW:], v_all[:, -WINDOW:]


def short_conv(x, buf, w):
    T = x.shape[1]
    xp = jnp.concatenate([buf, x], axis=1)
    y = xp[:, 0:T] * w[0]
    for i in range(1, CONV_W):
        y = y + xp[:, i:i + T] * w[i]
    return jax.nn.silu(y), xp[:, -(CONV_W - 1):]


def gdn_chunked(q, k, v, log_a, beta, S0, chunk):
    B, T, H, DK = q.shape
    DV = v.shape[-1]
    nc = T // chunk

    def blocks(t):
        t = t.reshape((B, nc, chunk) + t.shape[2:])
        return jnp.moveaxis(t, 2, 3)

    qc, kc, vc, bc = blocks(q), blocks(k), blocks(v), blocks(beta)
    g = jnp.cumsum(blocks(log_a), axis=-1)
    i = jnp.arange(chunk)
    incl = i[:, None] >= i[None, :]
    strict = i[:, None] > i[None, :]
    decay = jnp.exp(jnp.where(incl, g[..., :, None] - g[..., None, :], -jnp.inf))
    kk = jnp.einsum('bnhid,bnhjd->bnhij', kc, kc)
    A = jnp.where(strict, bc[..., :, None] * kk * decay, 0.0) + jnp.eye(chunk, dtype=q.dtype)
    rhs = jnp.concatenate([vc * bc[..., None], kc * (bc * jnp.exp(g))[..., None]], axis=-1)
    sol = lax.linalg.triangular_solve(A, rhs, left_side=True, lower=True, unit_diagonal=True)
    u, w = sol[..., :DV], sol[..., DV:]
    qk = jnp.einsum('bnhid,bnhjd->bnhij', qc, kc) * decay
    q_dec = qc * jnp.exp(g)[..., None]
    k_dec = kc * jnp.exp(g[..., -1:] - g)[..., None]
    g_tot = jnp.exp(g[..., -1])

    def step(S, xs):
        u_n, w_n, qk_n, qd_n, kd_n, gt_n = xs
        v_new = u_n - jnp.einsum('bhid,bhde->bhie', w_n, S)
        o = jnp.einsum('bhid,bhde->bhie', qd_n, S) + jnp.einsum('bhij,bhje->bhie', qk_n, v_new)
        S = S * gt_n[..., None, None] + jnp.einsum('bhid,bhie->bhde', kd_n, v_new)
        return S, o

    xs = (jnp.moveaxis(u, 1, 0), jnp.moveaxis(w, 1, 0), jnp.moveaxis(qk, 1, 0),
          jnp.moveaxis(q_dec, 1, 0), jnp.moveaxis(k_dec, 1, 0), jnp.moveaxis(g_tot, 1, 0))
    S, o = lax.scan(step, S0, xs)
    o = jnp.moveaxis(jnp.moveaxis(o, 0, 1), 3, 2).reshape(B, T, H, DV)
    return o, S


def gated_delta_branch(q, k, v, z, a, b, conv_buf, S0, conv_w, a_log, dt_bias, norm_g):
    B, T, _ = q.shape
    qkv, conv_new = short_conv(jnp.concatenate([q, k, v], axis=-1), conv_buf, conv_w)
    qh, kh, vh = split_cols(qkv, (GDN_QK, GDN_QK, GDN_V))
    qh = l2norm(qh.reshape(B, T, GDN_HEADS, GDN_DK).astype(jnp.float32)) * (GDN_DK ** -0.5)
    kh = l2norm(kh.reshape(B, T, GDN_HEADS, GDN_DK).astype(jnp.float32))
    vh = vh.reshape(B, T, GDN_HEADS, GDN_DV).astype(jnp.float32)
    beta = jax.nn.sigmoid(b.astype(jnp.float32))
    log_a = -jnp.exp(a_log.astype(jnp.float32)) * jax.nn.softplus(a.astype(jnp.float32) + dt_bias.astype(jnp.float32))
    o, S = gdn_chunked(qh, kh, vh, log_a, beta, S0.astype(jnp.float32), min(T, CHUNK))
    o = rmsnorm(o, norm_g) * jax.nn.silu(z.reshape(B, T, GDN_HEADS, GDN_DV).astype(jnp.float32))
    return o.reshape(B, T, GDN_V).astype(q.dtype), S.astype(S0.dtype), conv_new


def memory_kv(mem, g_mem, w_mem_kv):
    B = mem.shape[0]
    k, v = split_cols(rmsnorm(mem, g_mem) @ w_mem_kv, (MEM_Q, MEM_Q))
    return (k.reshape(B, N_MEM, MEM_HEADS, MEM_HEAD_DIM), v.reshape(B, N_MEM, MEM_HEADS, MEM_HEAD_DIM))


def memory_attention(q, mem_k, mem_v):
    B, T = q.shape[:2]
    s = jnp.einsum('bqhd,bmhd->bhqm', q, mem_k).astype(jnp.float32) * (MEM_HEAD_DIM ** -0.5)
    p = jax.nn.softmax(s, axis=-1)
    o = jnp.einsum('bhqm,bmhd->bqhd', p.astype(mem_v.dtype), mem_v)
    return o.reshape(B, T, MEM_Q)


def hybrid_layer(x, pos, k_hist, v_hist, hist_valid, S0, conv_buf, mem_k, mem_v,
                 g_pre_mix, w_in, b_in, swa_sinks, conv_w, gdn_a_log, gdn_dt_bias, gdn_norm_g,
                 w_branch, w_out, g_post_mix, g_pre_ffn, w_up, w_down, g_post_ffn):
    B, T, _ = x.shape
    h = rmsnorm(x, g_pre_mix)
    proj = h @ w_in + b_in
    qa, ka, va, qb, kb, vb, zb, ab, bb, qc, gl = split_cols(proj, SPLIT_SIZES)
    qa = rope(qa.reshape(B, T, SWA_HEADS, SWA_HEAD_DIM), pos)
    ka = rope(ka.reshape(B, T, SWA_KV_HEADS, SWA_HEAD_DIM), pos)
    va = va.reshape(B, T, SWA_KV_HEADS, SWA_HEAD_DIM)
    o_a, k_new, v_new = swa_attention(qa, ka, va, k_hist, v_hist, hist_valid, swa_sinks)
    o_b, S_new, conv_new = gated_delta_branch(qb, kb, vb, zb, ab, bb, conv_buf, S0, conv_w,
                                              gdn_a_log, gdn_dt_bias, gdn_norm_g)
    o_c = memory_attention(qc.reshape(B, T, MEM_HEADS, MEM_HEAD_DIM), mem_k, mem_v)
    branches = jnp.stack([o_a, o_b, o_c], axis=2)
    br = jnp.einsum('btnc,ncd->btnd', branches, w_branch)
    gates = jax.nn.sigmoid(gl.reshape(B, T, N_BRANCH, D_MODEL).astype(jnp.float32)).astype(x.dtype)
    merged = jnp.sum(gates * br, axis=2)
    x = x + rmsnorm(merged @ w_out, g_post_mix)
    hf = rmsnorm(x, g_pre_ffn)
    f = jnp.square(jax.nn.relu(hf @ w_up)) @ w_down
    x = x + rmsnorm(f, g_post_ffn)
    return x, k_new, v_new, S_new, conv_new


def setup_inputs(seed: int = 0) -> dict:
    key = jax.random.key(seed)
    ks = jax.random.split(key, 32)
    f32 = jnp.float32

    def nrm(k, shape, scale):
        return jax.random.normal(k, shape, f32) * scale

    def gain(k, shape):
        return 1.0 + 0.05 * jax.random.normal(k, shape, f32)

    dt = jnp.exp(jax.random.uniform(ks[9], (DEPTH, GDN_HEADS), f32, math.log(1e-3), math.log(1e-1)))
    return {
        'x_prompt': nrm(ks[0], (BATCH, SEQ, D_MODEL), 1.0),
        'x_sample': nrm(ks[1], (DEC_BATCH, DEC_SEQ, D_MODEL), 1.0),
        'mem_prompt': nrm(ks[2], (BATCH, N_MEM, D_MODEL), 1.0),
        'cache_swa_k': nrm(ks[3], (DEPTH, DEC_BATCH, WINDOW, SWA_KV_HEADS, SWA_HEAD_DIM), 1.0),
        'cache_swa_v': nrm(ks[4], (DEPTH, DEC_BATCH, WINDOW, SWA_KV_HEADS, SWA_HEAD_DIM), 1.0),
        'state_gdn': nrm(ks[5], (DEPTH, DEC_BATCH, GDN_HEADS, GDN_DK, GDN_DV), 0.1),
        'state_conv': nrm(ks[6], (DEPTH, DEC_BATCH, CONV_W - 1, CONV_CH), 1.0),
        'cache_mem_k': nrm(ks[7], (DEPTH, DEC_BATCH, N_MEM, MEM_HEADS, MEM_HEAD_DIM), 1.0),
        'cache_mem_v': nrm(ks[8], (DEPTH, DEC_BATCH, N_MEM, MEM_HEADS, MEM_HEAD_DIM), 1.0),
        'g_pre_mix': gain(ks[10], (DEPTH, D_MODEL)),
        'w_in': nrm(ks[11], (DEPTH, D_MODEL, D_IN), D_MODEL ** -0.5),
        'b_in': nrm(ks[12], (DEPTH, D_IN), 0.02),
        'swa_sinks': nrm(ks[13], (DEPTH, SWA_HEADS), 1.0),
        'conv_w': nrm(ks[14], (DEPTH, CONV_W, CONV_CH), CONV_W ** -0.5),
        'gdn_a_log': jnp.log(jax.random.uniform(ks[15], (DEPTH, GDN_HEADS), f32, 1.0, 16.0)),
        'gdn_dt_bias': dt + jnp.log(-jnp.expm1(-dt)),
        'gdn_norm_g': gain(ks[16], (DEPTH, GDN_DV)),
        'g_mem': gain(ks[17], (DEPTH, D_MODEL)),
        'w_mem_kv': nrm(ks[18], (DEPTH, D_MODEL, 2 * MEM_Q), D_MODEL ** -0.5),
        'w_branch': nrm(ks[19], (DEPTH, N_BRANCH, D_MODEL, D_MODEL), D_MODEL ** -0.5),
        'w_out': nrm(ks[20], (DEPTH, D_MODEL, D_MODEL), D_MODEL ** -0.5),
        'g_post_mix': gain(ks[21], (DEPTH, D_MODEL)),
        'g_pre_ffn': gain(ks[22], (DEPTH, D_MODEL)),
        'w_up': nrm(ks[23], (DEPTH, D_MODEL, D_FF), D_MODEL ** -0.5),
        'w_down': nrm(ks[24], (DEPTH, D_FF, D_MODEL), D_FF ** -0.5),
        'g_post_ffn': gain(ks[25], (DEPTH, D_MODEL)),
    }


def reference(x_prompt, x_sample, mem_prompt, cache_swa_k, cache_swa_v, state_gdn, state_conv,
              cache_mem_k, cache_mem_v, g_pre_mix, w_in, b_in, swa_sinks, conv_w, gdn_a_log,
              gdn_dt_bias, gdn_norm_g, g_mem, w_mem_kv, w_branch, w_out, g_post_mix, g_pre_ffn,
              w_up, w_down, g_post_ffn):
    xp, xs = x_prompt, x_sample
    Bp, Tp = xp.shape[0], xp.shape[1]
    Ts = xs.shape[1]
    pos_p = jnp.arange(Tp, dtype=jnp.int32)
    pos_s = PAST_LEN + jnp.arange(Ts, dtype=jnp.int32)
    zero_kv = jnp.zeros((Bp, WINDOW, SWA_KV_HEADS, SWA_HEAD_DIM), xp.dtype)
    zero_S = jnp.zeros((Bp, GDN_HEADS, GDN_DK, GDN_DV), xp.dtype)
    zero_conv = jnp.zeros((Bp, CONV_W - 1, CONV_CH), xp.dtype)
    p_k, p_v, p_S, p_c, p_mk, p_mv = [], [], [], [], [], []
    s_k, s_v, s_S, s_c = [], [], [], []
    for l in range(DEPTH):
        mk, mv = memory_kv(mem_prompt, g_mem[l], w_mem_kv[l])
        xp, kp_, vp_, Sp_, cp_ = hybrid_layer(
            xp, pos_p, zero_kv, zero_kv, False, zero_S, zero_conv, mk, mv,
            g_pre_mix[l], w_in[l], b_in[l], swa_sinks[l], conv_w[l], gdn_a_log[l], gdn_dt_bias[l],
            gdn_norm_g[l], w_branch[l], w_out[l], g_post_mix[l], g_pre_ffn[l], w_up[l], w_down[l], g_post_ffn[l])
        xs, ks_, vs_, Ss_, cs_ = hybrid_layer(
            xs, pos_s, cache_swa_k[l], cache_swa_v[l], True, state_gdn[l], state_conv[l],
            cache_mem_k[l], cache_mem_v[l],
            g_pre_mix[l], w_in[l], b_in[l], swa_sinks[l], conv_w[l], gdn_a_log[l], gdn_dt_bias[l],
            gdn_norm_g[l], w_branch[l], w_out[l], g_post_mix[l], g_pre_ffn[l], w_up[l], w_down[l], g_post_ffn[l])
        p_k.append(kp_); p_v.append(vp_); p_S.append(Sp_); p_c.append(cp_); p_mk.append(mk); p_mv.append(mv)
        s_k.append(ks_); s_v.append(vs_); s_S.append(Ss_); s_c.append(cs_)
    return (xp, xs,
            jnp.stack(p_k), jnp.stack(p_v), jnp.stack(p_S), jnp.stack(p_c), jnp.stack(p_mk), jnp.stack(p_mv),
            jnp.stack(s_k), jnp.stack(s_v), jnp.stack(s_S), jnp.stack(s_c))
```

```python
import math
from contextlib import ExitStack
import numpy as np
import concourse.bass as bass
import concourse.mybir as mybir
from concourse.bass_utils import run_bass_kernel_spmd

F32 = mybir.dt.float32
BF16 = mybir.dt.bfloat16
AF = mybir.ActivationFunctionType
ALU = mybir.AluOpType
AX = mybir.AxisListType

D_MODEL = 1024
SEQ = 16384
DEC_BATCH = 16
DEC_SEQ = 64
PAST_LEN = 2048
N_MEM = 256
EPS = 1e-6
ROPE_THETA = 10000.0
NCORES = 8
NT = 256
NEG = -30000.0
SWA_Q, SWA_KV, GDN_QK, GDN_V, MEM_Q = 1024, 128, 1024, 1024, 1024
OFF_QA = 0
OFF_KA = 1024
OFF_VA = 1152
OFF_QB = 1280
OFF_KB = 2304
OFF_VB = 3328
OFF_ZB = 4352
OFF_AB = 5376
OFF_BB = 5384
OFF_QC = 5392
OFF_GL = 6416
D_IN = 9488

import os as _os0
EPOCH = int(_os0.environ.get('DEV_EPOCH', '12000'))
ENGS = ("pe", "act", "dve", "pool", "sp")


class Region:
    __slots__ = ("name", "lw", "rd", "dsem", "dcnt", "excl")

    def __init__(self, name):
        self.name = name
        self.excl = False
        self.lw = None
        self.rd = []
        self.dsem = None
        self.dcnt = 0


class View:
    __slots__ = ("ap", "regs")

    def __init__(self, ap, regs):
        self.ap = ap
        self.regs = regs

    def __getitem__(self, idx):
        return View(self.ap[idx], self.regs)

    def re(self, pattern_, **kw):
        return View(self.ap.rearrange(pattern_, **kw), self.regs)

    def bc(self, shape):
        return View(self.ap.to_broadcast(list(shape)), self.regs)

    def us(self, axis):
        return View(self.ap.unsqueeze(axis), self.regs)

    def cast(self, dt):
        return View(self.ap.bitcast(dt), self.regs)


class Buf:
    def __init__(self, fw, name, shape, dtype, space="sbuf", nreg=1):
        self.name = name
        self.shape = list(shape)
        if space == "sbuf":
            self.t = fw.es.enter_context(fw.nc.sbuf_tensor(name, self.shape, dtype))
            fw.sbuf_bytes += int(np.prod(self.shape[1:])) * (2 if dtype == BF16 else 4)
        elif space == "psum":
            self.t = fw.es.enter_context(fw.nc.psum_tensor(name, self.shape, dtype))
        else:
            self.t = fw.nc.dram_tensor(name, self.shape, dtype, kind=space)
        self.regs = [Region(f"{name}.{i}") for i in range(nreg)]
        if space == "psum":
            for r_ in self.regs:
                r_.excl = True

    def __getitem__(self, idx):
        return View(self.t[idx], self.regs)

    def r(self, i):
        return View(self.t[:, i], [self.regs[i]])

    def rr(self, i0, i1):
        return View(self.t[:, i0:i1], self.regs[i0:i1])


class FW:
    def __init__(self, nc, es, nsem_dma=100):
        self.nc = nc
        self.es = es
        self.prog = {e: [] for e in ENGS}
        self.cnt = {e: 0 for e in ENGS}
        self.pending = {e: 0 for e in ENGS}
        self.esems = {e: [] for e in ENGS}
        self.seen = {e: {} for e in ENGS}
        self.dma_sems = []
        self.nsem_dma = nsem_dma
        self.final_dma = []
        self.capture = None
        self.all_dma_regs = []
        self.nwaits = 0
        self.nops = {e: 0 for e in ENGS}
        self.sbuf_bytes = 0

    def _esem(self, eng, epoch):
        lst = self.esems[eng]
        while len(lst) <= epoch:
            lst.append(self.es.enter_context(self.nc.semaphore(f"c_{eng}_{len(lst)}")))
        return lst[epoch]

    def _dsem(self, reg):
        if reg.dsem is None:
            if len(self.dma_sems) >= self.nsem_dma:
                raise RuntimeError("out of dma semaphores: " + reg.name)
            s = self.es.enter_context(self.nc.semaphore(f"d_{len(self.dma_sems)}"))
            self.dma_sems.append(s)
            reg.dsem = s
            self.all_dma_regs.append(reg)
        return reg.dsem

    def _wait_for(self, eng, tok, out):
        if tok is None:
            return
        if tok[0] == "e":
            _, e2, n = tok
            ep, v = divmod(n - 1, EPOCH)
            key = ("e", e2, ep)
            val = v + 1
            sem = self._esem(e2, ep)
        else:
            _, reg, c = tok
            key = ("d", id(reg))
            val = 16 * c
            sem = self._dsem(reg)
        seen = self.seen[eng]
        if seen.get(key, 0) >= val:
            return
        seen[key] = val
        out.append((sem, val))

    def _deps(self, eng, reads, writes):
        waits = []
        for v in reads:
            for r in v.regs:
                self._wait_for(eng, r.lw, waits)
                if r.excl:
                    for t in r.rd:
                        if t[0] != "e" or t[1] != eng:
                            self._wait_for(eng, t, waits)
        for v in writes:
            for r in v.regs:
                if not (eng == "pe" and r.lw is not None and r.lw[0] == "e" and r.lw[1] == "pe"):
                    self._wait_for(eng, r.lw, waits)
                for t in r.rd:
                    if t[0] == "e" and t[1] == eng:
                        continue
                    self._wait_for(eng, t, waits)
        return waits

    def _record(self, tok, reads, writes):
        for v in reads:
            for r in v.regs:
                if not r.rd or r.rd[-1] != tok:
                    r.rd.append(tok)
        for v in writes:
            for r in v.regs:
                r.lw = tok
                r.rd = []

    def op(self, eng, fn, reads=(), writes=(), inc=True):
        if self.capture is not None:
            self.capture.append(("op", eng, fn, list(reads), list(writes), inc))
            return
        waits = self._deps(eng, reads, writes)
        n = self.cnt[eng] + 1
        tok = ("e", eng, n)
        self._record(tok, reads, writes)
        self.nops[eng] += 1
        self.nwaits += len(waits)
        if inc:
            self.cnt[eng] = n
            sem = self._esem(eng, (n - 1) // EPOCH)
            self.pending[eng] = 0
        else:
            sem = None
            self.pending[eng] += 1

        def emit(e, fn=fn, waits=waits, sem=sem):
            for s, v in waits:
                e.wait_ge(s, v)
            ins = fn(e)
            if sem is not None:
                ins.then_inc(sem, 1)
        self.prog[eng].append(emit)
        return tok

    def dma(self, q, out, in_, final=False, **kw):
        if self.capture is not None:
            self.capture.append(("dma", q, out, in_, final, kw))
            return
        waits = self._deps(q, [in_], [out])
        wreg = out.regs[0]
        sem = self._dsem(wreg)
        wreg.dcnt += 1
        tok = ("d", wreg, wreg.dcnt)
        self._record(tok, [in_], [out])
        self.nops[q] += 1
        self.nwaits += len(waits)
        if final:
            self.final_dma.append(tok)

        def emit(e, out=out, in_=in_, waits=waits, sem=sem, kw=kw):
            for s, v in waits:
                e.wait_ge(s, v)
            e.dma_start(out=out.ap, in_=in_.ap, **kw).then_inc(sem, 16)
        self.prog[q].append(emit)
        return tok

    def replay(self, items):
        assert self.capture is None
        for it in items:
            if it[0] == "op":
                self.op(it[1], it[2], it[3], it[4], it[5])
            else:
                self.dma(it[1], it[2], it[3], final=it[4], **it[5])

    @staticmethod
    def interleave(a, b):
        def atoms(lst):
            out, cur, open_pe = [], [], False
            for it in lst:
                cur.append(it)
                if it[0] == "op" and it[1] == "pe":
                    open_pe = not it[5]
                if not open_pe:
                    out.append(cur)
                    cur = []
            if cur:
                out.append(cur)
            return out
        A, B = atoms(a), atoms(b)
        res, i, j = [], 0, 0
        while i < len(A) or j < len(B):
            if j >= len(B) or (i < len(A) and i * len(B) <= j * len(A)):
                res.extend(A[i]); i += 1
            else:
                res.extend(B[j]); j += 1
        return res

    def finish(self):
        for e in ENGS:
            assert self.pending[e] == 0, (e, self.pending[e])
        waits = []
        for t in self.final_dma:
            self._wait_for("sp", t, waits)
        for reg in self.all_dma_regs:
            self._wait_for("sp", ("d", reg, reg.dcnt), waits)
        self.prog["sp"].append(lambda e: [e.wait_ge(s, v) for s, v in waits])
        block = self.es.enter_context(self.nc.Block())
        prog = self.prog

        @block.tensor
        def _(e):
            for f in prog["pe"]:
                f(e)

        @block.scalar
        def _(e):
            for f in prog["act"]:
                f(e)

        @block.vector
        def _(e):
            for f in prog["dve"]:
                f(e)

        @block.gpsimd
        def _(e):
            for f in prog["pool"]:
                f(e)

        @block.sync
        def _(e):
            for f in prog["sp"]:
                f(e)


def unit_cols(w, cols):
    K = w.shape[0] // 128
    return np.ascontiguousarray(w[:, cols].reshape(K, 128, 128).transpose(1, 0, 2))


def swa_q_cols(j, swapped):
    cols = []
    for h in (j, j + 8):
        base = OFF_QA + 64 * h
        idx = np.arange(64)
        if swapped:
            idx = (idx + 32) % 64
        cols.append(base + idx)
    return np.concatenate(cols)


def swa_k_cols(swapped):
    cols = []
    for kv in range(2):
        idx = np.arange(64)
        if swapped:
            idx = (idx + 32) % 64
        cols.append(OFF_KA + 64 * kv + idx)
    return np.concatenate(cols)


def main_units_spec():
    spec = []
    for j in range(8):
        spec.append(("qa%d" % j, ("win", swa_q_cols(j, False))))
        spec.append(("qas%d" % j, ("win", swa_q_cols(j, True))))
    spec.append(("ka", ("win", swa_k_cols(False))))
    spec.append(("kas", ("win", swa_k_cols(True))))
    spec.append(("va", ("win", OFF_VA + np.arange(128))))
    for b in range(8):
        spec.append(("qc%d" % b, ("win", OFF_QC + 128 * b + np.arange(128))))
    for h in range(8):
        spec.append(("qb%d" % h, ("win", OFF_QB + 128 * h + np.arange(128))))
        spec.append(("kb%d" % h, ("win", OFF_KB + 128 * h + np.arange(128))))
        spec.append(("vb%d" % h, ("win", OFF_VB + 128 * h + np.arange(128))))
    for h in range(8):
        spec.append(("zb%d" % h, ("win", OFF_ZB + 128 * h + np.arange(128))))
    for ob in range(8):
        for n in range(3):
            spec.append(("gl%d_%d" % (n, ob), ("win", OFF_GL + 1024 * n + 128 * ob + np.arange(128))))
            spec.append(("br%d_%d" % (n, ob), ("wbr", n, ob)))
    for ob in range(8):
        spec.append(("wo%d" % ob, ("wout", ob)))
    for fb in range(32):
        spec.append(("up%d" % fb, ("wup", fb)))
    for ob in range(8):
        for kq in range(4):
            spec.append(("dn%d_%d" % (ob, kq), ("wdn", ob, kq)))
    return spec


BIAS_NAMES = (["qa%d" % j for j in range(8)] + ["qas%d" % j for j in range(8)] + ["ka", "kas"]
              + ["qc%d" % b for b in range(8)] + ["qb%d" % h for h in range(8)] + ["kb%d" % h for h in range(8)]
              + ["vb%d" % h for h in range(8)] + ["zb%d" % h for h in range(8)]
              + ["gl%d_%d" % (n, ob) for n in range(3) for ob in range(8)])
BIAS_COL = {n: i for i, n in enumerate(BIAS_NAMES)}


def rope_tables(pos):
    half = 32
    inv = ROPE_THETA ** (-np.arange(half, dtype=np.float32) / half)
    ang = pos.astype(np.float32)[None, :] * inv[:, None]
    ang = ang.astype(np.float32)
    cos = np.cos(ang).astype(np.float32)
    sin = np.sin(ang).astype(np.float32)
    cosT = np.concatenate([cos, cos, cos, cos], 0)
    sinT = np.concatenate([-sin, sin, -sin, sin], 0)
    return np.ascontiguousarray(cosT), np.ascontiguousarray(sinT)


def gdn_consts():
    p = np.arange(128)
    blk = p // 64
    loc = p % 64
    j = np.arange(64)
    c = {}
    c["ident"] = np.eye(128, dtype=np.float32)
    same = (blk[:, None] == blk[None, :])
    c["uincl_bd"] = (same & (loc[:, None] <= loc[None, :])).astype(np.float32)
    c["ones_bd"] = same.astype(np.float32)
    c["ustrict_bd"] = (same & (loc[:, None] > loc[None, :])).astype(np.float32)
    c["onesA"] = np.repeat((p < 64)[:, None], 128, 1).astype(np.float32)
    c["onesB"] = np.repeat((p >= 64)[:, None], 128, 1).astype(np.float32)
    c["ust"] = (loc[:, None] <= j[None, :]).astype(np.float32)
    c["nust"] = -c["ust"]
    c["maskLo"] = np.where(j[None, :] <= loc[:, None], 0.0, NEG).astype(np.float32)
    c["maskUp"] = np.where(j[None, :] >= loc[:, None], 0.0, NEG).astype(np.float32)
    c["negstrict"] = np.where(j[None, :] < loc[:, None], -1.0, 0.0).astype(np.float32)
    c["ident_st"] = (j[None, :] == loc[:, None]).astype(np.float32)
    bd = np.zeros((128, 2, 64), np.float32)
    bd[p < 64, 0, :] = 1.0
    bd[p >= 64, 1, :] = 1.0
    c["bdmask"] = bd.reshape(128, 128)
    return c


CST_ORDER = ["ident", "uincl_bd", "ones_bd", "ustrict_bd", "onesA", "onesB", "bdmask", "ust", "nust", "maskLo", "maskUp",
             "negstrict", "ident_st"]


def cst_layout():
    c = gdn_consts()
    off = {}
    o = 0
    for n in CST_ORDER:
        off[n] = (o, c[n].shape[1])
        o += c[n].shape[1]
    arr = np.concatenate([c[n] for n in CST_ORDER], 1)
    return off, np.ascontiguousarray(arr)


def swa_masks():
    q = np.arange(128)
    k = np.arange(256)
    std = np.zeros((128, 256), np.float32)
    a = q < 64
    std[np.ix_(a, k >= 192)] = NEG
    std[np.ix_(~a, k < 64)] = NEG
    samp = np.zeros((128, 256), np.float32)
    samp[np.ix_(a, k >= 192)] = NEG
    samp[np.ix_(~a, (k >= 128) & (k < 192))] = NEG
    return std, samp


def prep_inputs(inp, TP):
    f32 = np.float32
    NMG = TP // NT
    NWG = 7 * NMG
    xp = np.asarray(inp["x_prompt"], f32)[0]
    xs = np.asarray(inp["x_sample"], f32)
    w_in = np.asarray(inp["w_in"], f32)[0]
    b_in = np.asarray(inp["b_in"], f32)[0]
    w_br = np.asarray(inp["w_branch"], f32)[0]
    w_out = np.asarray(inp["w_out"], f32)[0]
    w_up = np.asarray(inp["w_up"], f32)[0]
    w_dn = np.asarray(inp["w_down"], f32)[0]
    w_mem = np.asarray(inp["w_mem_kv"], f32)[0]

    units = []
    for name, kind in main_units_spec():
        if kind[0] == "win":
            units.append(unit_cols(w_in, kind[1]))
        elif kind[0] == "wbr":
            units.append(unit_cols(w_br[kind[1]], 128 * kind[2] + np.arange(128)))
        elif kind[0] == "wout":
            units.append(unit_cols(w_out, 128 * kind[1] + np.arange(128)))
        elif kind[0] == "wup":
            units.append(unit_cols(w_up, 128 * kind[1] + np.arange(128)))
        elif kind[0] == "wdn":
            ob, kq = kind[1], kind[2]
            units.append(unit_cols(w_dn[1024 * kq:1024 * (kq + 1)], 128 * ob + np.arange(128)))
    wmain = np.stack(units, 0).reshape(len(units), 128, 1024)
    wwarm = np.stack([unit_cols(w_in, off + 128 * h + np.arange(128)) for h in range(8) for off in (OFF_KB, OFF_VB)],
                     0).reshape(16, 128, 1024)
    whalo = np.stack([unit_cols(w_in, swa_k_cols(False)), unit_cols(w_in, swa_k_cols(True)),
                      unit_cols(w_in, OFF_VA + np.arange(128))], 0).reshape(3, 128, 1024)
    wmem = np.ascontiguousarray(w_mem.reshape(8, 128, 8, 256).transpose(2, 1, 0, 3)).reshape(8, 128, 2048)
    wab = np.ascontiguousarray(w_in[:, OFF_AB:OFF_AB + 16].reshape(8, 128, 16).transpose(1, 0, 2)).reshape(128, 128)

    bfm = np.zeros((128, len(BIAS_NAMES)), f32)
    spec = dict(main_units_spec())
    for n, i in BIAS_COL.items():
        bfm[:, i] = b_in[spec[n][1]]
    rep = lambda v: np.ascontiguousarray(np.broadcast_to(np.asarray(v, f32)[None, :], (128, len(v))))
    sinks = np.asarray(inp["swa_sinks"], f32)[0]
    conv_w = np.asarray(inp["conv_w"], f32)[0]
    cw = np.ascontiguousarray(conv_w.reshape(4, 24, 128).transpose(2, 1, 0))
    smalls = np.concatenate([
        bfm,
        rep(b_in[OFF_VA:OFF_VA + 128]),
        rep(b_in[OFF_AB:OFF_AB + 16]),
        rep(sinks),
        rep(np.asarray(inp["gdn_a_log"], f32)[0]),
        rep(np.asarray(inp["gdn_dt_bias"], f32)[0]),
        np.asarray(inp["gdn_norm_g"], f32)[0][:, None],
        cw.reshape(128, 96),
    ], 1)
    gains = np.stack([rep(np.asarray(inp[k], f32)[0]) for k in
                      ("g_pre_mix", "g_post_mix", "g_pre_ffn", "g_post_ffn", "g_mem")], 0)
    cst_off, cst = cst_layout()
    mstd, msamp = swa_masks()
    mem_x = np.asarray(inp["mem_prompt"], f32)[0]
    ck = np.asarray(inp["cache_swa_k"], f32)[0].reshape(DEC_BATCH, 128, 128)
    cv = np.asarray(inp["cache_swa_v"], f32)[0].reshape(DEC_BATCH, 128, 128)
    sg = np.asarray(inp["state_gdn"], f32)[0]
    sc = np.asarray(inp["state_conv"], f32)[0]
    cmk = np.asarray(inp["cache_mem_k"], f32)[0].reshape(DEC_BATCH, 256, 1024)
    cmv = np.asarray(inp["cache_mem_v"], f32)[0].reshape(DEC_BATCH, 256, 1024)
    cs_s, sn_s = rope_tables(PAST_LEN + (np.arange(128) % 64))
    maps = []
    for c in range(NCORES):
        xw = np.zeros((7 * TP, 1024), f32)
        if c > 0:
            xw[(7 - c) * TP:] = xp[:c * TP]
        flags = np.zeros((128, NWG + 1), f32)
        flags[:, (7 - c) * NMG:NWG] = 1.0
        flags[:, NWG] = 1.0 if c > 0 else 0.0
        xh = xw[-128:].copy()
        pos_m = c * TP + np.arange(TP)
        cs_m, sn_m = rope_tables(pos_m)
        cs_h, sn_h = rope_tables(c * TP - 128 + np.arange(128))
        mfirst = mstd.copy()
        if c == 0:
            mfirst[:, :128] = NEG
        s0, s1 = 2 * c, 2 * c + 1
        sch = np.ascontiguousarray(sc[s0:s1 + 1].reshape(2, 3, 24, 128).transpose(3, 2, 0, 1))
        m = {
            "xw": xw.reshape(NWG, NT, 1024), "xm": np.ascontiguousarray(xp[c * TP:(c + 1) * TP]).reshape(NMG, NT, 1024),
            "xs": np.ascontiguousarray(xs[s0:s1 + 1]).reshape(128, 1024), "xh": xh, "flags": flags,
            "rope_m": np.ascontiguousarray(np.stack([cs_m, sn_m], 0)), "rope_h": np.stack([cs_h, sn_h], 0),
            "rope_s": np.stack([cs_s, sn_s], 0),
            "masks": np.stack([mstd, mfirst, msamp], 0),
            "cst": cst, "smalls": np.ascontiguousarray(smalls), "gains": gains,
            "wmain": wmain, "wwarm": wwarm, "whalo": whalo, "wmem": wmem, "wab": wab,
            "memx": mem_x,
            "ck": np.ascontiguousarray(ck[s0:s1 + 1]), "cv": np.ascontiguousarray(cv[s0:s1 + 1]),
            "sg": np.ascontiguousarray(sg[s0:s1 + 1].transpose(0, 2, 1, 3)),
            "sch": sch.reshape(128, 24 * 2 * 3),
            "cmk": np.ascontiguousarray(cmk[s0:s1 + 1]), "cmv": np.ascontiguousarray(cmv[s0:s1 + 1]),
        }
        maps.append(m)
    return maps


SM_OFF = {}
_o = 0
for _n, _w in (("bfm", len(BIAS_NAMES)), ("bva", 128), ("bab", 16), ("sinks", 16), ("alog", 8), ("dtb", 8),
               ("gng", 1), ("cw", 96)):
    SM_OFF[_n] = (_o, _w)
    _o += _w
SM_W = _o


class K:
    def __init__(self, TP):
        self.TP = TP
        self.NMG = TP // NT
        self.NWG = 7 * self.NMG
        self.nc = bass.Bass("TRN2", target_bir_lowering=False)
        self.es = ExitStack()
        self.fw = FW(self.nc, self.es)
        self.psn = 0
        self.ps_set = (0, 6)

    def sb(self, name, shape, dt=F32, nreg=1):
        return Buf(self.fw, "s_" + name, shape, dt, "sbuf", nreg)

    def din(self, name, shape):
        return Buf(self.fw, name, shape, F32, "ExternalInput")

    def dout(self, name, shape):
        return Buf(self.fw, name, shape, F32, "ExternalOutput")

    def ps(self, n=1):
        lo, nb = self.ps_set
        if n == 1:
            b = lo + self.psn % nb
            self.psn += 1
            return self.psum.r(b)
        if self.psn % 2:
            self.psn += 1
        b = lo + self.psn % nb
        self.psn += 2
        return self.psum.rr(b, b + 2)

    def ps_long(self, n=1):
        if n == 1:
            return self.psum.r(6)
        return self.psum.rr(6, 8)

    def _sc(self, s, reads):
        if isinstance(s, View):
            reads.append(s)
            return s.ap
        return s

    def tt(self, eng, out, a, b, op):
        self.fw.op(eng, lambda e: e.tensor_tensor(out=out.ap, in0=a.ap, in1=b.ap, op=op), [a, b], [out])

    def ts(self, eng, out, a, s1, op0, s2=None, op1=None):
        reads = [a]
        s1a = self._sc(s1, reads)
        s2a = self._sc(s2, reads)
        if op1 is None:
            self.fw.op(eng, lambda e: e.tensor_scalar(out=out.ap, in0=a.ap, scalar1=s1a, scalar2=None, op0=op0),
                       reads, [out])
        else:
            self.fw.op(eng, lambda e: e.tensor_scalar(out=out.ap, in0=a.ap, scalar1=s1a, scalar2=s2a, op0=op0, op1=op1),
                       reads, [out])

    def stt(self, eng, out, a, s, b, op0, op1):
        reads = [a, b]
        sa = self._sc(s, reads)
        self.fw.op(eng, lambda e: e.scalar_tensor_tensor(out=out.ap, in0=a.ap, scalar=sa, in1=b.ap, op0=op0, op1=op1),
                   reads, [out])

    def act(self, out, a, func, bias=None, scale=None, accum=None):
        reads = [a]
        writes = [out]
        kw = {}
        if bias is not None:
            kw["bias"] = self._sc(bias, reads)
        if scale is not None:
            kw["scale"] = self._sc(scale, reads)
        if accum is not None:
            kw["accum_out"] = accum.ap
            writes.append(accum)
        self.fw.op("act", lambda e: e.activation(out=out.ap, in_=a.ap, func=func, **kw), reads, writes)

    def cp(self, eng, out, a):
        if eng == "act":
            self.act(out, a, AF.Copy)
        else:
            self.fw.op(eng, lambda e: e.tensor_copy(out=out.ap, in_=a.ap), [a], [out])

    def memset(self, out, val):
        self.fw.op("dve", lambda e: e.memset(out.ap, val), [], [out])

    def recip(self, out, a):
        self.fw.op("dve", lambda e: e.reciprocal(out=out.ap, in_=a.ap), [a], [out])

    def rmax(self, out, a):
        self.fw.op("dve", lambda e: e.reduce_max(out=out.ap, in_=a.ap, axis=AX.X), [a], [out])

    def mm(self, out, lhsT, rhs, start=True, stop=True, inc=None):
        if inc is None:
            inc = stop
        self.fw.op("pe", lambda e: e.matmul(out.ap, lhsT=lhsT.ap, rhs=rhs.ap, start=start, stop=stop),
                   [lhsT, rhs], [out], inc=inc)

    def tr(self, out, a, inc=True):
        idn = self.ident
        self.fw.op("pe", lambda e: e.transpose(out.ap, a.ap, idn.ap), [a, idn], [out], inc=inc)

    def dma(self, q, out, a, **kw):
        self.fw.dma(q, out, a, **kw)


class WStream:
    NR = 6
    UPS = 2

    def __init__(self, k):
        self.k = k
        self.ring = k.sb("wring", [128, self.NR, self.UPS, 1024], BF16, nreg=self.NR)
        self.loads = []
        self.issued = 0
        self.cur_load = 0
        self.cur_unit = 0

    def add(self, dbuf, u0, n):
        i = u0
        while i < u0 + n:
            m = min(self.UPS, u0 + n - i)
            self.loads.append((View(dbuf.t[i:i + m].rearrange("u p e -> p u e"), dbuf.regs), m))
            i += m

    def _issue_upto(self, L):
        while self.issued <= min(L, len(self.loads) - 1):
            v, m = self.loads[self.issued]
            slot = self.issued % self.NR
            for u_ in range(m):
                self.k.dma("pool", View(self.ring.t[:, slot, u_, :], [self.ring.regs[slot]]), v[:, u_, :])
            self.issued += 1

    def next(self):
        self._issue_upto(self.cur_load + self.NR - 1)
        v, m = self.loads[self.cur_load]
        slot = self.cur_load % self.NR
        u = View(self.ring.t[:, slot, self.cur_unit, :].rearrange("p (k c) -> p k c", k=8), [self.ring.regs[slot]])
        self.cur_unit += 1
        if self.cur_unit == m:
            self.cur_unit = 0
            self.cur_load += 1
        return u

    def next_load_raw(self):
        assert self.cur_unit == 0
        self._issue_upto(self.cur_load + self.NR - 1)
        slot = self.cur_load % self.NR
        u = View(self.ring.t[:, slot].rearrange("p u e -> p (u e)"), [self.ring.regs[slot]])
        self.cur_load += 1
        return u


class _Stop(Exception):
    pass


def build(TP, debug=None, stop=None, nwarm=None):
    import os as _os
    k = K(TP)
    fw = k.fw
    NMG, NWG = k.NMG, k.NWG
    NU = len(main_units_spec())
    cst_off, _ = cst_layout()
    CW = sum(w for _, w in cst_off.values())

    d_xw = k.din("xw", [NWG, NT, 1024])
    d_xm = k.din("xm", [NMG, NT, 1024])
    d_xs = k.din("xs", [128, 1024])
    d_xh = k.din("xh", [128, 1024])
    d_flags = k.din("flags", [128, NWG + 1])
    d_rope_m = k.din("rope_m", [2, 128, TP])
    d_rope_h = k.din("rope_h", [2, 128, 128])
    d_rope_s = k.din("rope_s", [2, 128, 128])
    d_masks = k.din("masks", [3, 128, 256])
    d_cst = k.din("cst", [128, CW])
    d_smalls = k.din("smalls", [128, SM_W])
    d_gains = k.din("gains", [5, 128, 1024])
    d_wmain = k.din("wmain", [NU, 128, 1024])
    d_wwarm = k.din("wwarm", [16, 128, 1024])
    d_whalo = k.din("whalo", [3, 128, 1024])
    d_wmem = k.din("wmem", [8, 128, 2048])
    d_wab = k.din("wab", [128, 128])
    d_memx = k.din("memx", [256, 1024])
    d_ck = k.din("ck", [2, 128, 128])
    d_cv = k.din("cv", [2, 128, 128])
    d_sg = k.din("sg", [2, 128, 8, 128])
    d_sch = k.din("sch", [128, 144])
    d_cmk = k.din("cmk", [2, 256, 1024])
    d_cmv = k.din("cmv", [2, 256, 1024])

    o_y = k.dout("y", [NMG, NT, 1024])
    o_ys = k.dout("ys", [128, 1024])
    o_pkT = k.dout("pkT", [128, 128])
    o_pv = k.dout("pv", [128, 128])
    o_pS = k.dout("pS", [128, 8, 128])
    o_pcT = k.dout("pcT", [128, 24, 3])
    o_pmk = k.dout("pmk", [256, 1024])
    o_pmv = k.dout("pmv", [256, 1024])
    o_skT = k.dout("skT", [128, 128])
    o_sv = k.dout("sv", [128, 128])
    o_skc = k.dout("skc", [2, 64, 128])
    o_svc = k.dout("svc", [2, 64, 128])
    o_sS = k.dout("sS", [2, 128, 8, 128])
    o_scT = k.dout("scT", [128, 24, 2, 3])
    dbg = {}
    if debug:
        for name, shape in debug.items():
            dbg[name] = k.dout("dbg_" + name, shape)

    k.psum = Buf(fw, "psum", [128, 8, 512], F32, "psum", nreg=8)
    cst = k.sb("cst", [128, CW])
    smalls = k.sb("smalls", [128, SM_W])
    flags = k.sb("flagsb", [128, NWG + 1])
    masks = k.sb("masksb", [128, 3, 256], BF16)
    wab = k.sb("wabb", [128, 8, 16], BF16)
    gb = k.sb("gb", [128, 2, 1024], F32, nreg=2)
    k.gbn = 0
    st = k.sb("st", [128, 8])
    junk = k.sb("junk", [128, 1024], BF16)
    xn = k.sb("xn", [128, 1024])
    oa = xn
    xt = k.sb("xt", [128, 2, 1024], F32, nreg=2)
    hT = k.sb("hT", [128, 8, NT], BF16)
    ones_full = k.sb("ones_full", [128, 128])
    ident_bf = k.sb("ident_bf", [128, 128], BF16)
    nsink = k.sb("nsink", [128, 16])
    negA = k.sb("negA", [128, 8])
    gkv = k.sb("gkv", [128, 16, NT])
    gq = k.sb("gq", [128, 8, NT])
    go = k.sb("go", [128, 8, NT])
    b4a = k.sb("b4a", [128, 8, NT], BF16)
    b4b = k.sb("b4b", [128, 8, NT], BF16)
    oaT = k.sb("oaT", [128, 8, NT], BF16)
    obT = k.sb("obT", [128, 8, NT], BF16)
    ocT = k.sb("ocT", [128, 8, NT], BF16)
    kTg, vTg = gkv[:, 0:8, :], gkv[:, 8:16, :]
    actT = View(gkv.t[:].rearrange("p a c -> p (a c)").bitcast(BF16).rearrange("p (f c) -> p f c", f=32), gkv.regs)
    qTg, moT, oTg, pf = gq, gq, go, go
    mk_tm = View(gq.t[:].rearrange("p a c -> p (a c)").rearrange("p (m e) -> p m e", m=2), gq.regs)
    qTa, mrg, mqT, hfT = b4a, b4a, b4b, b4b
    mkT = k.sb("mkT", [128, 8, 256], BF16)
    mv = k.sb("mv", [128, 2, 1024], BF16)
    ropeb = k.sb("ropeb", [128, 2, NT])
    kTa = k.sb("kTa", [128, 128 + NT], BF16)
    kTf = k.sb("kTf", [128, NT])
    vtm = k.sb("vtm", [128, 3, 128], BF16)
    vtf = k.sb("vtf", [128, 2, 128])
    tmp1 = k.sb("tmp1", [128, NT])
    tmp2 = k.sb("tmp2", [128, NT])
    kTc = k.sb("kTc", [128, 2, 128], BF16)
    vc = k.sb("vc", [128, 2, 128], BF16)
    ckf = k.sb("ckf", [128, 2, 128])
    mx = k.sb("mx", [128, 16])
    negm = k.sb("negm", [128, 16])
    se = k.sb("se", [128, 16])
    esk = k.sb("esk", [128, 16])
    pT = k.sb("pT", [128, 2, 2, 128], BF16, nreg=2)
    k.ptn = 0
    gt3 = k.sb("gt", [128, 3, NT], F32, nreg=3)
    tm3 = k.sb("tm3", [128, NT])
    rl2 = k.sb("rl", [128, 2, NT], BF16, nreg=2)
    S_p = k.sb("S_p", [128, 8, 128])
    S_a = k.sb("S_a", [128, 8, 128])
    hist = k.sb("hist", [128, 24, 3])
    hist_s = k.sb("hist_s", [128, 24, 2, 3])
    halo_hT = k.sb("halo_hT", [128, 8, 3], BF16)
    xpb = k.sb("xpb", [128, 2, NT + 6], F32, nreg=2)
    k.xpn = 0
    cacc2 = k.sb("cacc", [128, 2, NT], F32, nreg=2)
    csil2 = k.sb("csil", [128, 2, NT], F32, nreg=2)
    rnb2 = k.sb("rnb", [128, 2, NT], F32, nreg=2)
    cacc, csil, rnb = cacc2.r(0), csil2.r(0), rnb2.r(0)
    ab = k.sb("ab", [128, 16])
    bet = k.sb("bet", [128, 8])
    la = k.sb("la", [128, 8])
    nla = k.sb("nla", [128, 8])
    gg = k.sb("gg", [128, 32])
    EE = k.sb("EE", [128, 32])
    bkg = k.sb("bkg", [128, 8])
    tq3 = k.sb("tq3", [128, 3])
    g4 = [k.sb("g4_%d" % i, [128, 8, 128]) for i in range(8)]
    g2 = [k.sb("g2_%d" % i, [128, 8, 64]) for i in range(7)]
    N_bd = Tt_bd = g4[0]
    U_bd = vbt = g4[1]
    kbg, kdt, wT, vnew, qdT = g4[2], g4[3], g4[5], g4[6], g4[7]
    ut = g4[4]
    egrow = View(g4[4].t[:].rearrange("p h j -> p (h j)").rearrange("p (c h j) -> p c h j", c=2, h=8), g4[4].regs)
    Dm, DTm, N_st, laU, qkT = g2[0], g2[1], g2[4], g2[5], g2[6]
    t1 = U_st = g2[2]
    nbs = Tt_st = g2[3]

    def bfview(buf, n):
        flat = buf.t[:].rearrange("p h j -> p (h j)").bitcast(BF16)
        return View(flat[:, 0:8 * n].rearrange("p (h j) -> p h j", h=8), buf.regs)
    N_bdb = bfview(g2[0], 128)
    N_stb = bfview(g2[1], 64)
    U_stb = bfview(g2[2], 64)
    Tt_stb = bfview(g2[3], 64)
    U_bdb = bfview(g4[1], 128)
    Tt_bdb = bfview(g4[0], 128)
    vbtb = bfview(g4[1], 128)
    kbgb = bfview(g4[2], 128)
    kdtb = bfview(g4[3], 128)
    wTb = bfview(g4[5], 128)
    vnewb = bfview(g4[6], 128)
    qdTb = bfview(g4[7], 128)
    qkTb = bfview(g2[6], 64)
    Sb = bfview(g2[0], 128)

    def C(name):
        o, w = cst_off[name]
        return cst[:, o:o + w]

    def SM(name, a=None, b=None):
        o, w = SM_OFF[name]
        if a is None:
            return smalls[:, o:o + w]
        return smalls[:, o + a:o + b]

    def BIAS(name):
        c = BIAS_COL[name]
        return SM("bfm", c, c + 1)

    k.ident = C("ident")
    k.dma("sp", cst[:], d_cst[:])
    k.dma("sp", smalls[:], d_smalls[:])
    k.dma("sp", flags[:], d_flags[:])
    for m_ in range(3):
        k.dma("pool", masks[:, m_, :], d_masks[m_])
    k.dma("pool", wab[:].re("p k c -> p (k c)"), d_wab[:])
    k.memset(ones_full[:], 1.0)
    ones_bf = k.sb("ones_bf", [128, 128], BF16)
    k.memset(ones_bf[:], 1.0)
    k.cp("dve", ident_bf[:], C("ident"))
    k.ts("dve", nsink[:], SM("sinks"), -1.0, ALU.mult)
    k.act(negA[:], SM("alog"), AF.Exp)
    k.ts("dve", negA[:], negA[:], -1.0, ALU.mult)
    k.memset(S_p[:], 0.0)
    k.memset(hist[:], 0.0)

    def gain(i):
        s = k.gbn % 2
        k.gbn += 1
        k.dma("sp", gb.r(s), d_gains[i])
        return gb.r(s)

    ws = WStream(k)
    ws.loads = [(View(d_wmem.t[i].rearrange("p (u e) -> p u e", u=2), d_wmem.regs), 2) for i in range(8)]
    ws.add(d_whalo, 0, 3)
    for g in range(NWG):
        ws.add(d_wwarm, 0, 16)
    for g in range(NMG + 1):
        ws.add(d_wmain, 0, NU)
    if _os.environ.get('DEV_NOPF'):
        ws.loads = ws.loads[:int(_os.environ['DEV_NOPF'])]

    def dump(name, v, bf=False):
        if name in dbg:
            k.dma("pool" if bf else "sp", dbg[name][:], v, final=True)

    def transpose_to(dst, src_tm, t):
        for half in range(2):
            p = k.ps(1)
            for kk in range(4):
                c = (half * 4 + kk) * 128
                k.tr(p[:, kk * 128:(kk + 1) * 128], src_tm[:, c:c + 128], inc=(kk == 3))
            k.cp("act" if half == 0 else "dve", dst[:, half * 4:(half + 1) * 4, t * 128:(t + 1) * 128],
                 p.re("p (k c) -> p k c", k=4))

    def norm_rows(xr, gv, col):
        k.act(junk[:], xr, AF.Square, accum=st[:, col:col + 1])
        k.act(st[:, col + 1:col + 2], st[:, col:col + 1], AF.Sqrt, bias=EPS, scale=1.0 / 1024)
        k.recip(st[:, col + 2:col + 3], st[:, col + 1:col + 2])
        k.stt("dve", xn[:], xr, st[:, col + 2:col + 3], gv, ALU.mult, ALU.mult)

    def load_norm(src_rows, ntile, gain_idx, dst=None):
        gv = gain(gain_idx)
        for t in range(ntile):
            xr = xt.r(t)
            k.dma("sp", xr, src_rows(t))
            norm_rows(xr, gv, 0)
            transpose_to(hT if dst is None else dst, xn, t)

    def proj_fm(u, ncol, src=None):
        s = hT if src is None else src
        p = k.ps(1)
        for kc in range(8):
            k.mm(p[:, 0:ncol], u[:, kc, :], s[:, kc, 0:ncol], start=(kc == 0), stop=(kc == 7))
        return p[:, 0:ncol]

    def build_mkT(tm):
        for blk in range(8):
            p = k.ps(1)
            for mt in range(2):
                k.tr(p[:, mt * 128:(mt + 1) * 128], tm[:, mt, blk * 128:(blk + 1) * 128], inc=(mt == 1))
            k.cp("act" if blk % 2 else "dve", mkT[:, blk, :], p[:, 0:256])

    def chk(tag):
        if stop == tag:
            raise _Stop()

    def _body():
        chk('setup')
        load_norm(lambda t: d_memx[t * 128:(t + 1) * 128, :], 2, 4)
        chk('mem_ln')
        for piece in range(int(_os.environ.get('DEV_PIECES', '8'))):
            wv = ws.next_load_raw().re("p (k c) -> p k c", k=8)
            q4 = piece % 4
            for mt in range(2):
                p = k.ps(1)
                for kc in range(8):
                    k.mm(p[:, 0:256], hT[:, kc, mt * 128:(mt + 1) * 128], wv[:, kc, :], start=(kc == 0), stop=(kc == 7))
                if piece < 4:
                    k.cp("act", mk_tm[:, mt, q4 * 256:(q4 + 1) * 256], p[:, 0:256])
                else:
                    k.cp("act", xt.r(mt)[:, q4 * 256:(q4 + 1) * 256], p[:, 0:256])
                    k.cp("dve", mv[:, mt, q4 * 256:(q4 + 1) * 256], p[:, 0:256])
        chk('mem_mm')
        for mt in range(2):
            k.dma("sp", o_pmk[mt * 128:(mt + 1) * 128, :], mk_tm[:, mt, :], final=True)
            k.dma("sp", o_pmv[mt * 128:(mt + 1) * 128, :], xt.r(mt), final=True)
        build_mkT(mk_tm)
        chk('mem')

        def rope_evac(p, p_sw, bname, bsname, ncol, out_bf, out_f32=None):
            k.stt("dve", tmp1[:, 0:ncol], p, BIAS(bname), ropeb[:, 0, 0:ncol], ALU.add, ALU.mult)
            k.stt("dve", tmp2[:, 0:ncol], p_sw, BIAS(bsname), ropeb[:, 1, 0:ncol], ALU.add, ALU.mult)
            if out_f32 is not None:
                k.tt("dve", out_f32, tmp1[:, 0:ncol], tmp2[:, 0:ncol], ALU.add)
                k.cp("act", out_bf, out_f32)
            else:
                k.tt("dve", out_bf, tmp1[:, 0:ncol], tmp2[:, 0:ncol], ALU.add)

        def v_tm(u, ntile, tile0):
            for t in range(ntile):
                p = k.ps(1)
                for kc in range(8):
                    k.mm(p[:, 0:128], hT[:, kc, t * 128:(t + 1) * 128], u[:, kc, :], start=(kc == 0), stop=(kc == 7))
                k.tt("dve", vtf[:, t, :], p[:, 0:128], SM("bva"), ALU.add)
                k.cp("act", vtm[:, tile0 + t, :], vtf[:, t, :])

        load_norm(lambda t: d_xh[:, :], 1, 0)
        k.dma("sp", ropeb[:, :, 0:128], d_rope_h[:].re("m p c -> p m c"))
        u_k = ws.next()
        pk0 = proj_fm(u_k, 128)
        u_ks = ws.next()
        pks0 = proj_fm(u_ks, 128)
        rope_evac(pk0, pks0, "ka", "kas", 128, kTa[:, 0:128])
        v_tm(ws.next(), 1, 0)
        chk('halo')

        def swa_tile(qcols, keyspecs, pvspecs, mask_v, t_idx):
            for kv in range(2):
                rows = slice(64 * kv, 64 * kv + 64)
                hs = slice(kv * 8, kv * 8 + 8)
                pss = []
                for j in range(8):
                    h = kv * 8 + j
                    if j % 2 == 0:
                        pb = k.ps(1)
                    ps_h = pb[:, (j % 2) * 256:(j % 2 + 1) * 256]
                    pss.append(ps_h)
                    k.mm(ps_h, ident_bf[:], mask_v, start=True, stop=False, inc=False)
                    specs_ = keyspecs(kv)
                    for si_, (orow, ocol, qsub, rhs) in enumerate(specs_):
                        last_ = (si_ == len(specs_) - 1)
                        k.mm(ps_h[orow, ocol], qTa[rows, j, qcols][:, qsub], rhs, start=False, stop=last_, inc=last_)
                    k.rmax(mx[:, h:h + 1], ps_h)
                k.ts("dve", negm[:, hs], mx[:, hs], -0.125, ALU.mult)
                k.tt("dve", negm[:, hs], negm[:, hs], nsink[:, hs], ALU.min)
                for j in range(8):
                    h = kv * 8 + j
                    k.act(pf[:, j, :], pss[j], AF.Exp, bias=negm[:, h:h + 1], scale=0.125, accum=se[:, h:h + 1])
                k.tt("dve", esk[:, hs], negm[:, hs], SM("sinks")[:, hs], ALU.add)
                k.act(esk[:, hs], esk[:, hs], AF.Exp)
                k.tt("dve", se[:, hs], se[:, hs], esk[:, hs], ALU.add)
                k.recip(se[:, hs], se[:, hs])
                po = k.ps_long(1)
                po3 = po.re("p (h d) -> p h d", h=8)
                for j in range(8):
                    s = k.ptn % 2
                    k.ptn += 1
                    pt_ps = k.ps(1)
                    for kt in range(2):
                        k.tr(pt_ps[:, kt * 128:(kt + 1) * 128], pf[:, j, kt * 128:(kt + 1) * 128], inc=(kt == 1))
                    ptv = View(pT.t[:, s], [pT.regs[s]])
                    k.cp("act" if j % 2 else "dve", ptv, pt_ps[:, 0:256].re("p (t q) -> p t q", t=2))
                    groups = pvspecs(kv)
                    for gi_, lst in enumerate(groups):
                        for i, (orow, kt, lsub, rhs) in enumerate(lst):
                            l = ptv[:, kt, :] if lsub is None else ptv[:, kt, lsub]
                            last = (i == len(lst) - 1)
                            k.mm(po3[orow, j, :], l, rhs, start=(i == 0), stop=last,
                                 inc=(last and gi_ == len(groups) - 1))
                k.tt("dve", oa[:, kv * 512:(kv + 1) * 512].re("p (h d) -> p h d", h=8), po3, bcl(se[:, hs], 64), ALU.mult)
            transpose_to(oaT, oa, t_idx)

        def mem_tile(qc, rows, t_idx, finish=True):
            pss = []
            for h in range(4):
                if h % 2 == 0:
                    pb = k.ps(1)
                ps_h = pb[rows, (h % 2) * 256:(h % 2 + 1) * 256]
                pss.append(ps_h)
                for c in range(2):
                    k.mm(ps_h, mqT[:, 2 * h + c, qc][:, rows], mkT[:, 2 * h + c, :], start=(c == 0), stop=(c == 1))
                k.rmax(mx[rows, h:h + 1], ps_h)
            k.ts("dve", negm[rows, 0:4], mx[rows, 0:4], -1.0 / 16, ALU.mult)
            for h in range(4):
                k.act(pf[rows, h, :], pss[h], AF.Exp, bias=negm[rows, h:h + 1], scale=1.0 / 16, accum=se[rows, h:h + 1])
            k.recip(se[rows, 0:4], se[rows, 0:4])
            po2 = k.ps_long(2)
            pov = po2.re("p b (h d) -> p (b h) d", h=2)
            for h in range(4):
                s = k.ptn % 2
                k.ptn += 1
                pt_ps = k.ps(1)
                for kt in range(2):
                    k.tr(pt_ps[:, kt * 128:(kt + 1) * 128], pf[:, h, kt * 128:(kt + 1) * 128], inc=(kt == 1))
                ptv = View(pT.t[:, s], [pT.regs[s]])
                k.cp("act" if h % 2 else "dve", ptv, pt_ps[:, 0:256].re("p (t q) -> p t q", t=2))
                for kt in range(2):
                    k.mm(pov[rows, h, :], ptv[:, kt, rows], mv[:, kt, h * 256:(h + 1) * 256], start=(kt == 0), stop=(kt == 1))
            k.tt("dve", oa[rows].re("p (h d) -> p h d", h=4), pov[rows], bcl(se[rows, 0:4], 256), ALU.mult)
            if finish:
                transpose_to(ocT, oa, t_idx)

        def bc8(v):
            n = v.ap.shape[-1]
            return v.us(1).bc([128, 8, n])

        def bcl(v, n):
            return v.us(2).bc([v.ap.shape[0], v.ap.shape[1], n])

        def gdn_conv(u, bname, blk, ncol, nseq, hist_v, dst, mode, flag=None, prefill=False, hsrc=None, dst_bf=None):
            L = ncol // nseq
            s = k.xpn % 2
            k.xpn += 1
            cacc, csil, rnb = cacc2.r(s), csil2.r(s), rnb2.r(s)
            xp = View(xpb.t[:, s, 0:nseq * (L + 3)].rearrange("p (s l) -> p s l", s=nseq), [xpb.regs[s]])
            if prefill:
                p3 = k.ps(1)
                for kc in range(8):
                    k.mm(p3[:, 0:3], u[:, kc, :], halo_hT[:, kc, :], start=(kc == 0), stop=(kc == 7))
                k.act(tq3[:], p3[:, 0:3], AF.Identity, bias=BIAS(bname), scale=1.0)
                k.ts("dve", hist_v, tq3[:].us(1), flags[:, NWG:NWG + 1], ALU.mult)
            p = proj_fm(u, ncol, src=hsrc)
            k.cp("dve", xp[:, :, 0:3], hist_v)
            k.act(xp[:, :, 3:3 + L], p.re("p (s l) -> p s l", s=nseq), AF.Identity, bias=BIAS(bname), scale=1.0)
            if flag is None:
                k.cp("dve", hist_v, xp[:, :, L:L + 3])
            else:
                k.ts("dve", hist_v, xp[:, :, L:L + 3], flag, ALU.mult)
            acc = cacc[:, 0:ncol].re("p (s l) -> p s l", s=nseq)
            cwv = lambda i: SM("cw", blk * 4 + i, blk * 4 + i + 1)
            k.act(acc, xp[:, :, 0:L], AF.Copy, scale=cwv(0))
            for i in range(1, 4):
                k.stt("dve", acc, xp[:, :, i:i + L], cwv(i), acc, ALU.mult, ALU.add)
            if mode == "v":
                k.act(dst, cacc[:, 0:ncol], AF.Silu)
                return
            k.act(csil[:, 0:ncol], cacc[:, 0:ncol], AF.Silu)
            sqb = View(cacc.ap.bitcast(BF16)[:, 0:ncol], cacc.regs)
            k.act(sqb, csil[:, 0:ncol], AF.Square)
            pn = k.ps(1)
            k.mm(pn[:, 0:ncol], ones_bf[:], sqb)
            k.act(rnb[:, 0:ncol], pn[:, 0:ncol], AF.Sqrt, bias=EPS, scale=1.0)
            k.recip(rnb[:, 0:ncol], rnb[:, 0:ncol])
            if mode == "k":
                k.tt("dve", dst, csil[:, 0:ncol], rnb[:, 0:ncol], ALU.mult)
                if dst_bf is not None:
                    k.cp("act", dst_bf, dst)
            else:
                k.stt("dve", dst, csil[:, 0:ncol], 128.0 ** -0.5, rnb[:, 0:ncol], ALU.mult, ALU.mult)

        def to_bd(dst, src):
            k.tt("dve", (dst if isinstance(dst, View) else dst[:]).re("p h (c j) -> p h c j", c=2),
                 (src if isinstance(src, View) else src[:]).us(2).bc([128, 8, 2, 64]),
                 C("bdmask").re("p (c j) -> p c j", c=2).us(1).bc([128, 8, 2, 64]), ALU.mult)

        def heads8(p2):
            return p2.re("p b (h j) -> p (b h) j", h=4)

        def gdn_tile2(col0, main, Sv, hsrc=None, ksrc=None, vsrc=None, kbsrc=None):
            kb_ = obT[:] if kbsrc is None else kbsrc
            hT_ = hT if hsrc is None else hsrc
            kT_ = kTg if ksrc is None else ksrc
            vT_ = vTg if vsrc is None else vsrc
            tc = slice(col0, col0 + 128)
            p = k.ps(1)
            for kc in range(8):
                k.mm(p[:, 0:16], hT_[:, kc, tc], wab[:, kc, :], start=(kc == 0), stop=(kc == 7))
            k.tt("dve", ab[:], p[:, 0:16], SM("bab"), ALU.add)
            k.act(bet[:], ab[:, 8:16], AF.Sigmoid)
            k.tt("dve", la[:], ab[:, 0:8], SM("dtb"), ALU.add)
            k.act(la[:], la[:], AF.Exp)
            k.act(la[:], la[:], AF.Ln, bias=1.0, scale=1.0)
            k.tt("dve", la[:], la[:], negA[:], ALU.mult)
            if main:
                k.ts("dve", nla[:], la[:], -1.0, ALU.mult)
            p = k.ps(1)
            k.mm(p[:, 0:8], C("uincl_bd"), la[:], inc=False)
            k.mm(p[:, 8:16], C("ustrict_bd"), la[:], inc=False)
            k.mm(p[:, 16:24], C("onesA"), la[:], inc=False)
            k.mm(p[:, 24:32], C("onesB"), la[:])
            k.act(EE[:], p[:, 0:32], AF.Exp)
            k.tt("dve", bkg[:], bet[:], EE[:, 0:8], ALU.mult)
            k.tt("dve", laU[:], bc8(C("ust")), bcl(la[:], 64), ALU.mult)
            k.tt("dve", t1[:], bc8(C("nust")), bcl(la[:], 64), ALU.mult)
            p = k.ps(1)
            p3 = p.re("p (h j) -> p h j", h=8)
            k.mm(p3, C("uincl_bd"), bcl(la[:], 64), start=True, stop=False, inc=False)
            k.mm(p3, C("ones_bd"), t1[:], start=False, stop=False, inc=False)
            k.mm(p3, C("ident"), bc8(C("maskLo")), start=False, stop=True)
            k.act(Dm[:], p3, AF.Exp)
            k.tt("dve", nbs[:], bc8(C("negstrict")), bcl(bet[:], 64), ALU.mult)
            k.tt("dve", t1[:], Dm[:], nbs[:], ALU.mult)
            if main:
                p = k.ps(1)
                p3 = p.re("p (h j) -> p h j", h=8)
                k.mm(p3, C("ones_bd"), laU[:], start=True, stop=False, inc=False)
                k.mm(p3, C("uincl_bd"), bcl(nla[:], 64), start=False, stop=False, inc=False)
                k.mm(p3, C("ident"), bc8(C("maskUp")), start=False, stop=True)
                k.act(DTm[:], p3, AF.Exp)
                p2 = k.ps(2)
                k.mm(p2[:, 0].re("p (h j) -> p h j", h=8), C("onesA"), laU[:], inc=False)
                k.mm(p2[:, 1].re("p (h j) -> p h j", h=8), C("onesB"), laU[:])
                k.act(egrow, p2.re("p c (h j) -> p c h j", h=8), AF.Exp)
            pkk3 = k.ps(1).re("p (h j) -> p h j", h=8)
            for h in range(8):
                for c in range(2):
                    r = slice(64 * c, 64 * c + 64)
                    cc = slice(col0 + 64 * c, col0 + 64 * c + 64)
                    k.mm(pkk3[r, h, :], kb_[:, h, cc], kb_[:, h, cc], inc=(h == 7 and c == 1))
            k.tt("dve", N_st[:], pkk3, t1[:], ALU.mult)
            if main:
                pkq3 = k.ps(1).re("p (h j) -> p h j", h=8)
                for h in range(8):
                    for c in range(2):
                        r = slice(64 * c, 64 * c + 64)
                        cc = slice(col0 + 64 * c, col0 + 64 * c + 64)
                        k.mm(pkq3[r, h, :], kT_[:, h, cc], qTg[:, h, cc], inc=(h == 7 and c == 1))
                k.tt("dve", qkTb, pkq3, DTm[:], ALU.mult)
                for c_ in range(2):
                    k.tt("dve", qdTb[:, :, 64 * c_:64 * c_ + 64], qTg[:, :, col0 + 64 * c_:col0 + 64 * c_ + 64],
                         egrow[:, c_], ALU.mult)
            p2v = heads8(k.ps(2))
            for h in range(8):
                k.tr(p2v[:, h, :], kT_[:, h, tc], inc=(h == 7))
            k.tt("dve", kbgb, p2v, bcl(bkg[:], 128), ALU.mult)
            k.tt("dve", kdtb, p2v, bcl(EE[:, 8:16], 128), ALU.mult)
            to_bd(N_bd, N_st)
            p2v = heads8(k.ps(2))
            for h in range(8):
                k.tr(p2v[:, h, :], N_bd[:, h, :], inc=(h == 7))
            k.cp("act", U_bdb, p2v)
            k.cp("dve", N_stb, N_st[:])
            to_bd(N_bdb, N_st)
            k.tt("dve", U_stb, U_bdb[:, :, 0:64], U_bdb[:, :, 64:128], ALU.add)
            k.tt("dve", Tt_stb, U_stb, bc8(C("ident_st")), ALU.add)
            for lvl in range(5):
                pn3 = k.ps(1).re("p (h j) -> p h j", h=8)
                for h in range(8):
                    k.mm(pn3[:, h, :], U_bdb[:, h, :], N_stb[:, h, :], inc=(h == 7))
                if lvl < 4:
                    pu3 = k.ps(1).re("p (h j) -> p h j", h=8)
                    for h in range(8):
                        k.mm(pu3[:, h, :], N_bdb[:, h, :], U_stb[:, h, :], inc=(h == 7))
                k.cp("act", N_stb, pn3)
                if lvl < 4:
                    to_bd(U_bdb, pu3)
                    k.cp("act", U_stb, pu3)
                to_bd(N_bdb, N_stb)
                pt3 = k.ps(1).re("p (h j) -> p h j", h=8)
                for h in range(8):
                    k.mm(pt3[:, h, :], N_bdb[:, h, :], Tt_stb[:, h, :], inc=(h == 7))
                k.tt("dve", Tt_stb, Tt_stb, pt3, ALU.add)
            p2v = heads8(k.ps(2))
            for h in range(8):
                k.tr(p2v[:, h, :], vT_[:, h, tc], inc=(h == 7))
            k.tt("dve", vbtb, p2v, bcl(bet[:], 128), ALU.mult)
            to_bd(Tt_bdb, Tt_stb)
            p2v = heads8(k.ps(2))
            for h in range(8):
                k.mm(p2v[:, h, :], Tt_bdb[:, h, :], vbtb[:, h, :], inc=(h == 7))
            k.cp("act", ut[:], p2v)
            p2v = heads8(k.ps(2))
            for h in range(8):
                k.mm(p2v[:, h, :], kbgb[:, h, :], Tt_bdb[:, h, :], inc=(h == 7))
            k.cp("act", wTb, p2v)
            if main:
                pov = heads8(k.ps_long(2))
            for c in range(2):
                r = slice(64 * c, 64 * c + 64)
                S = Sv[c]
                k.cp("act", Sb, S)
                pvv = heads8(k.ps(2))
                for h in range(8):
                    k.mm(pvv[r, h, :], wTb[:, h, r], Sb[:, h, :], inc=(h == 7))
                k.tt("dve", vnewb[r], ut[r], pvv[r], ALU.subtract)
                if main:
                    for h in range(8):
                        k.mm(pov[:, h, r], Sb[:, h, :], qdTb[:, h, r], start=True, stop=False, inc=False)
                        k.mm(pov[:, h, r], vnewb[r, h, :], qkTb[r, h, :], start=False, stop=True, inc=(h == 7))
                pdv = heads8(k.ps(2))
                for h in range(8):
                    k.mm(pdv[:, h, :], kdtb[r, h, :], vnewb[r, h, :], inc=(h == 7))
                k.tt("dve", S, S, bcl(EE[:, 16 + 8 * c:24 + 8 * c], 128), ALU.mult)
                k.tt("dve", S, S, pdv, ALU.add)
            if main:
                k.cp("act", oTg[:, :, tc], pov)

        hTw = [hT[:], b4a[:]]
        kTw = [kTg, gq[:]]
        vTw = [vTg, go[:]]
        kBw = [b4b[:], oaT[:]]
        NW_ = NWG if nwarm is None else nwarm

        def warm_B(g):
            par = g % 2
            load_norm(lambda t, g=g: d_xw[g, t * 128:(t + 1) * 128, :], 2, 0, dst=hTw[par])
            fl = flags[:, g:g + 1]
            for h in range(8):
                gdn_conv(ws.next(), "kb%d" % h, 8 + h, NT, 1, hist[:, 8 + h, :].us(1), kTw[par][:, h, :], "k", flag=fl,
                         hsrc=hTw[par], dst_bf=kBw[par][:, h, :])
                gdn_conv(ws.next(), "vb%d" % h, 16 + h, NT, 1, hist[:, 16 + h, :].us(1), vTw[par][:, h, :], "v", flag=fl,
                         hsrc=hTw[par])

        def warm_A(g):
            par = g % 2
            for t in range(2):
                gdn_tile2(t * 128, False, [S_p[:], S_p[:]], hsrc=hTw[par], ksrc=kTw[par], vsrc=vTw[par], kbsrc=kBw[par])
            k.ts("dve", S_p[:], S_p[:], flags[:, g:g + 1], ALU.mult)

        def capture(fn, g, ps_set):
            k.ps_set = ps_set
            fw.capture = []
            fn(g)
            lst = fw.capture
            fw.capture = None
            k.ps_set = (0, 6)
            return lst

        if NW_ > 0:
            fw.replay(capture(warm_B, 0, (4, 2)))
        for g in range(NW_):
            la_ = capture(warm_A, g, (0, 4))
            lb_ = capture(warm_B, g + 1, (4, 2)) if g + 1 < NW_ else []
            fw.replay(FW.interleave(la_, lb_))
            chk('warm1')
        k.cp("dve", halo_hT[:], hTw[(NW_ - 1) % 2][:, :, NT - 3:NT])
        dump("S_warm", S_p[:])
        chk('warm')

        def resid_norm(srcT, ntile, gain_idx, out_dram=None, next_gain=None):
            gv = gain(gain_idx)
            gv2 = gain(next_gain) if next_gain is not None else None
            for t in range(ntile):
                p2f = k.ps(2).re("p b c -> p (b c)")
                for kk in range(8):
                    k.tr(p2f[:, kk * 128:(kk + 1) * 128], srcT[:, kk, t * 128:(t + 1) * 128], inc=(kk == 7))
                norm_rows(p2f, gv, 0)
                xr = xt.r(t)
                k.tt("dve", xr, xr, xn[:], ALU.add)
                if out_dram is not None:
                    k.dma("sp", out_dram(t), xr, final=True)
                if gv2 is not None:
                    norm_rows(xr, gv2, 3)
                    transpose_to(hfT, xn, t)

        def main_group(gi, sample=False):
            ncol = 128 if sample else NT
            ntile = ncol // 128
            nseq = 2 if sample else 1
            A, B = slice(0, 64), slice(64, 128)
            if sample:
                load_norm(lambda t: d_xs[:, :], 1, 0)
                k.dma("sp", ropeb[:, :, 0:128], d_rope_s[:].re("m p c -> p m c"))
            else:
                load_norm(lambda t: d_xm[gi, t * 128:(t + 1) * 128, :], 2, 0)
                k.dma("sp", ropeb[:], d_rope_m[:, :, gi * NT:(gi + 1) * NT].re("m p c -> p m c"))
            for j in range(8):
                pq = proj_fm(ws.next(), ncol)
                pqs = proj_fm(ws.next(), ncol)
                rope_evac(pq, pqs, "qa%d" % j, "qas%d" % j, ncol, qTa[:, j, 0:ncol])
            pk_ = proj_fm(ws.next(), ncol)
            pks_ = proj_fm(ws.next(), ncol)
            rope_evac(pk_, pks_, "ka", "kas", ncol, kTa[:, 128:128 + ncol], out_f32=kTf[:, 0:ncol])
            v_tm(ws.next(), ntile, 1)
            if gi == 0 and not sample:
                dump("qTa", qTa[:].re("p k c -> p (k c)"), True)
                dump("kTa", kTa[:, 0:256], True)
            for b in range(8):
                pm = proj_fm(ws.next(), ncol)
                k.act(mqT[:, b, 0:ncol], pm, AF.Identity, bias=BIAS("qc%d" % b), scale=1.0)
            if sample:
                k.dma("sp", hist_s[:].re("p b s i -> p (b s i)"), d_sch[:])
                k.dma("sp", S_a[:], d_sg[0])
                k.dma("sp", S_p[:], d_sg[1])
            def stream_attn():
                if sample:
                    for s in range(2):
                        k.dma("sp", ckf[:, s, :], d_ck[s])
                        k.dma("pool", vc[:, s, :], d_cv[s])
                        p = k.ps(1)
                        k.tr(p[:, 0:128], ckf[:, s, :])
                        k.cp("dve", kTc[:, s, :], p[:, 0:128])

                    def keyspecs(kv):
                        r = slice(64 * kv, 64 * kv + 64)
                        return [(A, slice(0, 128), A, kTc[r, 0, :]), (B, slice(0, 128), B, kTc[r, 1, :]),
                                (slice(0, 128), slice(128, 256), slice(0, 128), kTa[r, 128:256])]

                    def pvspecs(kv):
                        c = slice(64 * kv, 64 * kv + 64)
                        return [[(A, 0, A, vc[:, 0, c]), (A, 1, A, vtm[:, 1, c])],
                                [(B, 0, B, vc[:, 1, c]), (B, 1, B, vtm[:, 1, c])]]
                    swa_tile(slice(0, 128), keyspecs, pvspecs, masks[:, 2, :], 0)
                else:
                    for t in range(ntile):
                        def keyspecs(kv, t=t):
                            r = slice(64 * kv, 64 * kv + 64)
                            return [(slice(0, 128), slice(0, 256), slice(0, 128), kTa[r, t * 128:t * 128 + 256])]

                        def pvspecs(kv, t=t):
                            c = slice(64 * kv, 64 * kv + 64)
                            return [[(slice(0, 128), 0, None, vtm[:, t, c]), (slice(0, 128), 1, None, vtm[:, t + 1, c])]]
                        mv_ = masks[:, 1, :] if (gi == 0 and t == 0) else masks[:, 0, :]
                        swa_tile(slice(t * 128, (t + 1) * 128), keyspecs, pvspecs, mv_, t)
                if sample:
                    k.dma("sp", o_skT[:], kTf[:, 0:128], final=True)
                    k.dma("sp", o_sv[:], vtf[:, 0, :], final=True)
                    for s in range(2):
                        k.dma("sp", o_skc[s], d_ck[s, 64:128, :], final=True)
                        k.dma("sp", o_svc[s], d_cv[s, 64:128, :], final=True)
                else:
                    if gi == NMG - 1:
                        k.dma("sp", o_pkT[:], kTf[:, NT - 128:NT], final=True)
                        k.dma("sp", o_pv[:], vtf[:, 1, :], final=True)
                    k.cp("dve", kTa[:, 0:128], kTa[:, NT:NT + 128])
                    k.cp("act", vtm[:, 0, :], vtm[:, 2, :])
                if gi == 0 and not sample:
                    dump("oaT", oaT[:].re("p k c -> p (k c)"), True)
                if sample:
                    for s in range(2):
                        rows = A if s == 0 else B
                        for mt in range(2):
                            k.dma("sp", mk_tm[:, mt, :], d_cmk[s, mt * 128:(mt + 1) * 128, :])
                            k.dma("pool", mv[:, mt, :], d_cmv[s, mt * 128:(mt + 1) * 128, :])
                        build_mkT(mk_tm)
                        mem_tile(slice(0, 128), rows, 0, finish=(s == 1))
                else:
                    for t in range(ntile):
                        mem_tile(slice(t * 128, (t + 1) * 128), slice(0, 128), t)
                if gi == 0 and not sample:
                    dump("ocT", ocT[:].re("p k c -> p (k c)"), True)

            def stream_conv():
                for h in range(8):
                    hv = (lambda b: hist_s[:, b, :, :]) if sample else (lambda b: hist[:, b, :].us(1))
                    gdn_conv(ws.next(), "qb%d" % h, h, ncol, nseq, hv(h), qTg[:, h, 0:ncol], "q",
                             prefill=(gi == 0 and not sample))
                    gdn_conv(ws.next(), "kb%d" % h, 8 + h, ncol, nseq, hv(8 + h), kTg[:, h, 0:ncol], "k",
                             dst_bf=obT[:, h, 0:ncol])
                    gdn_conv(ws.next(), "vb%d" % h, 16 + h, ncol, nseq, hv(16 + h), vTg[:, h, 0:ncol], "v")

            if sample or stop is not None:
                stream_attn()
                stream_conv()
            else:
                k.ps_set = (0, 5)
                fw.capture = []
                stream_attn()
                ly_ = fw.capture
                k.ps_set = (5, 1)
                fw.capture = []
                stream_conv()
                lx_ = fw.capture
                fw.capture = None
                k.ps_set = (0, 6)
                fw.replay(FW.interleave(ly_, lx_))
            for t in range(ntile):
                gdn_tile2(t * 128, True, [S_a[:], S_p[:]] if sample else [S_p[:], S_p[:]])
            if sample:
                k.dma("sp", o_sS[0], S_a[:], final=True)
                k.dma("sp", o_sS[1], S_p[:], final=True)
                k.dma("sp", o_scT[:], hist_s[:], final=True)
            elif gi == NMG - 1:
                k.dma("sp", o_pS[:], S_p[:], final=True)
                k.dma("sp", o_pcT[:], hist[:], final=True)
            if gi == 0 and not sample:
                dump("oTg", oTg[:].re("p k c -> p (k c)"))
            dump("qTg", qTg[:].re("p k c -> p (k c)"))
            chk('gdn')
            for h in range(8):
                sqb = View(cacc.ap.bitcast(BF16)[:, 0:ncol], cacc.regs)
                k.tt("dve", sqb, oTg[:, h, 0:ncol], oTg[:, h, 0:ncol], ALU.mult)
                pn = k.ps(1)
                k.mm(pn[:, 0:ncol], ones_bf[:], sqb)
                k.act(rnb[:, 0:ncol], pn[:, 0:ncol], AF.Sqrt, bias=EPS, scale=1.0 / 128)
                k.recip(rnb[:, 0:ncol], rnb[:, 0:ncol])
                k.stt("dve", cacc[:, 0:ncol], oTg[:, h, 0:ncol], SM("gng"), rnb[:, 0:ncol], ALU.mult, ALU.mult)
                pz = proj_fm(ws.next(), ncol)
                k.act(csil[:, 0:ncol], pz, AF.Silu, bias=BIAS("zb%d" % h), scale=1.0)
                k.tt("dve", obT[:, h, 0:ncol], cacc[:, 0:ncol], csil[:, 0:ncol], ALU.mult)
            if gi == 0 and not sample:
                dump("obT", obT[:].re("p k c -> p (k c)"), True)
            chk('gdnout')
            srcs = [oaT, obT, ocT]
            for ob in range(8):
                for n in range(3):
                    pg = proj_fm(ws.next(), ncol)
                    gt = gt3.r(n)
                    k.act(gt[:, 0:ncol], pg, AF.Sigmoid, bias=BIAS("gl%d_%d" % (n, ob)), scale=1.0)
                    pb_ = proj_fm(ws.next(), ncol, src=srcs[n])
                    tn_ = (tmp1, tmp2, tm3)[n]
                    k.tt("dve", tn_[:, 0:ncol], pb_, gt[:, 0:ncol], ALU.mult)
                k.tt("dve", tmp1[:, 0:ncol], tmp1[:, 0:ncol], tmp2[:, 0:ncol], ALU.add)
                k.tt("dve", mrg[:, ob, 0:ncol], tmp1[:, 0:ncol], tm3[:, 0:ncol], ALU.add)
            for ob in range(8):
                po_ = proj_fm(ws.next(), ncol, src=mrg)
                k.cp("act", moT[:, ob, 0:ncol], po_)
            if gi == 0 and not sample:
                dump("mrg", mrg[:].re("p k c -> p (k c)"), True)
                dump("moT", moT[:].re("p k c -> p (k c)"))
            resid_norm(moT, ntile, 1, next_gain=2)
            if gi == 0 and not sample:
                dump("x1", xt[:].re("p k c -> p (k c)"))
                dump("hfT", hfT[:].re("p k c -> p (k c)"), True)
            chk('merge')
            for fb in range(32):
                pu_ = proj_fm(ws.next(), ncol, src=hfT)
                rl = rl2.r(fb % 2)
                k.act(rl[:, 0:ncol], pu_, AF.Relu)
                k.tt("dve", actT[:, fb, 0:ncol], rl[:, 0:ncol], rl[:, 0:ncol], ALU.mult)
            for ob in range(8):
                p = k.ps(1)
                for kq in range(4):
                    u = ws.next()
                    for kc in range(8):
                        k.mm(p[:, 0:ncol], u[:, kc, :], actT[:, kq * 8 + kc, 0:ncol],
                             start=(kq == 0 and kc == 0), stop=(kq == 3 and kc == 7))
                k.cp("act", moT[:, ob, 0:ncol], p[:, 0:ncol])
            if sample:
                resid_norm(moT, 1, 3, out_dram=lambda t: o_ys[:, :])
            else:
                resid_norm(moT, 2, 3, out_dram=lambda t: o_y[gi, t * 128:(t + 1) * 128, :])

        for g in range(NMG):
            main_group(g)
            chk('main%d' % g)
        main_group(0, sample=True)

    try:
        _body()
    except _Stop:
        pass
    fw.finish()
    return k


def run(inp, TP, debug=None):
    maps = prep_inputs(inp, TP)
    kb = build(TP, debug)
    res = run_bass_kernel_spmd(kb.nc, maps, core_ids=list(range(NCORES)))
    return res.results, kb


def assemble(r, TP):
    y = np.concatenate([r[c]["y"].reshape(TP, 1024) for c in range(NCORES)], 0)[None]
    ys = np.concatenate([r[c]["ys"].reshape(2, 64, 1024) for c in range(NCORES)], 0)
    L = NCORES - 1
    p_k = np.ascontiguousarray(r[L]["pkT"].T).reshape(1, 1, 128, 2, 64)
    p_v = r[L]["pv"].reshape(1, 1, 128, 2, 64)
    p_S = np.ascontiguousarray(r[L]["pS"].transpose(1, 0, 2)).reshape(1, 1, 8, 128, 128)
    p_c = np.ascontiguousarray(r[L]["pcT"].transpose(2, 1, 0)).reshape(1, 1, 3, 3072)
    p_mk = r[0]["pmk"].reshape(1, 1, 256, 4, 256)
    p_mv = r[0]["pmv"].reshape(1, 1, 256, 4, 256)
    sk, sv, sS, sc = [], [], [], []
    for c in range(NCORES):
        kn = np.ascontiguousarray(r[c]["skT"].T).reshape(2, 64, 128)
        vn = r[c]["sv"].reshape(2, 64, 128)
        sk.append(np.concatenate([r[c]["skc"], kn], 1))
        sv.append(np.concatenate([r[c]["svc"], vn], 1))
        sS.append(r[c]["sS"].transpose(0, 2, 1, 3))
        sc.append(r[c]["scT"].transpose(2, 3, 1, 0).reshape(2, 3, 3072))
    s_k = np.concatenate(sk, 0).reshape(1, 16, 128, 2, 64)
    s_v = np.concatenate(sv, 0).reshape(1, 16, 128, 2, 64)
    s_S = np.ascontiguousarray(np.concatenate(sS, 0)).reshape(1, 16, 8, 128, 128)
    s_c = np.ascontiguousarray(np.concatenate(sc, 0)).reshape(1, 16, 3, 3072)
    outs = (y, ys, p_k, p_v, p_S, p_c, p_mk, p_mv, s_k, s_v, s_S, s_c)
    return tuple(np.ascontiguousarray(o, dtype=np.float32) for o in outs)


def kernel(**inputs):
    TP = SEQ // NCORES
    r, _ = run(inputs, TP)
    return assemble(r, TP)
```

```python
import math
from contextlib import ExitStack
import numpy as np
import concourse.bass as bass
import concourse.mybir as mybir
from concourse.bass_utils import run_bass_kernel_spmd

F32 = mybir.dt.float32
BF16 = mybir.dt.bfloat16
AF = mybir.ActivationFunctionType
ALU = mybir.AluOpType
AX = mybir.AxisListType

D_MODEL = 1024
SEQ = 16384
DEC_BATCH = 16
DEC_SEQ = 64
PAST_LEN = 2048
N_MEM = 256
EPS = 1e-6
ROPE_THETA = 10000.0
NCORES = 8
NT = 256
NEG = -30000.0
SWA_Q, SWA_KV, GDN_QK, GDN_V, MEM_Q = 1024, 128, 1024, 1024, 1024
OFF_QA = 0
OFF_KA = 1024
OFF_VA = 1152
OFF_QB = 1280
OFF_KB = 2304
OFF_VB = 3328
OFF_ZB = 4352
OFF_AB = 5376
OFF_BB = 5384
OFF_QC = 5392
OFF_GL = 6416
D_IN = 9488

import os as _os0
EPOCH = int(_os0.environ.get('DEV_EPOCH', '12000'))
ENGS = ("pe", "act", "dve", "pool", "sp")


class Region:
    __slots__ = ("name", "lw", "rd", "dsem", "dcnt", "excl")

    def __init__(self, name):
        self.name = name
        self.excl = False
        self.lw = None
        self.rd = []
        self.dsem = None
        self.dcnt = 0


class View:
    __slots__ = ("ap", "regs")

    def __init__(self, ap, regs):
        self.ap = ap
        self.regs = regs

    def __getitem__(self, idx):
        return View(self.ap[idx], self.regs)

    def re(self, pattern_, **kw):
        return View(self.ap.rearrange(pattern_, **kw), self.regs)

    def bc(self, shape):
        return View(self.ap.to_broadcast(list(shape)), self.regs)

    def us(self, axis):
        return View(self.ap.unsqueeze(axis), self.regs)

    def cast(self, dt):
        return View(self.ap.bitcast(dt), self.regs)


class Buf:
    def __init__(self, fw, name, shape, dtype, space="sbuf", nreg=1):
        self.name = name
        self.shape = list(shape)
        if space == "sbuf":
            self.t = fw.es.enter_context(fw.nc.sbuf_tensor(name, self.shape, dtype))
            fw.sbuf_bytes += int(np.prod(self.shape[1:])) * (2 if dtype == BF16 else 4)
        elif space == "psum":
            self.t = fw.es.enter_context(fw.nc.psum_tensor(name, self.shape, dtype))
        else:
            self.t = fw.nc.dram_tensor(name, self.shape, dtype, kind=space)
        self.regs = [Region(f"{name}.{i}") for i in range(nreg)]
        if space == "psum":
            for r_ in self.regs:
                r_.excl = True

    def __getitem__(self, idx):
        return View(self.t[idx], self.regs)

    def r(self, i):
        return View(self.t[:, i], [self.regs[i]])

    def rr(self, i0, i1):
        return View(self.t[:, i0:i1], self.regs[i0:i1])


class FW:
    def __init__(self, nc, es, nsem_dma=100):
        self.nc = nc
        self.es = es
        self.prog = {e: [] for e in ENGS}
        self.cnt = {e: 0 for e in ENGS}
        self.pending = {e: 0 for e in ENGS}
        self.esems = {e: [] for e in ENGS}
        self.seen = {e: {} for e in ENGS}
        self.dma_sems = []
        self.nsem_dma = nsem_dma
        self.final_dma = []
        self.capture = None
        self.all_dma_regs = []
        self.nwaits = 0
        self.nops = {e: 0 for e in ENGS}
        self.sbuf_bytes = 0

    def _esem(self, eng, epoch):
        lst = self.esems[eng]
        while len(lst) <= epoch:
            lst.append(self.es.enter_context(self.nc.semaphore(f"c_{eng}_{len(lst)}")))
        return lst[epoch]

    def _dsem(self, reg):
        if reg.dsem is None:
            if len(self.dma_sems) >= self.nsem_dma:
                raise RuntimeError("out of dma semaphores: " + reg.name)
            s = self.es.enter_context(self.nc.semaphore(f"d_{len(self.dma_sems)}"))
            self.dma_sems.append(s)
            reg.dsem = s
            self.all_dma_regs.append(reg)
        return reg.dsem

    def _wait_for(self, eng, tok, out):
        if tok is None:
            return
        if tok[0] == "e":
            _, e2, n = tok
            ep, v = divmod(n - 1, EPOCH)
            key = ("e", e2, ep)
            val = v + 1
            sem = self._esem(e2, ep)
        else:
            _, reg, c = tok
            key = ("d", id(reg))
            val = 16 * c
            sem = self._dsem(reg)
        seen = self.seen[eng]
        if seen.get(key, 0) >= val:
            return
        seen[key] = val
        out.append((sem, val))

    def _deps(self, eng, reads, writes):
        waits = []
        for v in reads:
            for r in v.regs:
                self._wait_for(eng, r.lw, waits)
                if r.excl:
                    for t in r.rd:
                        if t[0] != "e" or t[1] != eng:
                            self._wait_for(eng, t, waits)
        for v in writes:
            for r in v.regs:
                if not (eng == "pe" and r.lw is not None and r.lw[0] == "e" and r.lw[1] == "pe"):
                    self._wait_for(eng, r.lw, waits)
                for t in r.rd:
                    if t[0] == "e" and t[1] == eng:
                        continue
                    self._wait_for(eng, t, waits)
        return waits

    def _record(self, tok, reads, writes):
        for v in reads:
            for r in v.regs:
                if not r.rd or r.rd[-1] != tok:
                    r.rd.append(tok)
        for v in writes:
            for r in v.regs:
                r.lw = tok
                r.rd = []

    def op(self, eng, fn, reads=(), writes=(), inc=True):
        if self.capture is not None:
            self.capture.append(("op", eng, fn, list(reads), list(writes), inc))
            return
        waits = self._deps(eng, reads, writes)
        n = self.cnt[eng] + 1
        tok = ("e", eng, n)
        self._record(tok, reads, writes)
        self.nops[eng] += 1
        self.nwaits += len(waits)
        if inc:
            self.cnt[eng] = n
            sem = self._esem(eng, (n - 1) // EPOCH)
            self.pending[eng] = 0
        else:
            sem = None
            self.pending[eng] += 1

        def emit(e, fn=fn, waits=waits, sem=sem):
            for s, v in waits:
                e.wait_ge(s, v)
            ins = fn(e)
            if sem is not None:
                ins.then_inc(sem, 1)
        self.prog[eng].append(emit)
        return tok

    def dma(self, q, out, in_, final=False, **kw):
        if self.capture is not None:
            self.capture.append(("dma", q, out, in_, final, kw))
            return
        waits = self._deps(q, [in_], [out])
        wreg = out.regs[0]
        sem = self._dsem(wreg)
        wreg.dcnt += 1
        tok = ("d", wreg, wreg.dcnt)
        self._record(tok, [in_], [out])
        self.nops[q] += 1
        self.nwaits += len(waits)
        if final:
            self.final_dma.append(tok)

        def emit(e, out=out, in_=in_, waits=waits, sem=sem, kw=kw):
            for s, v in waits:
                e.wait_ge(s, v)
            e.dma_start(out=out.ap, in_=in_.ap, **kw).then_inc(sem, 16)
        self.prog[q].append(emit)
        return tok

    def replay(self, items):
        assert self.capture is None
        for it in items:
            if it[0] == "op":
                self.op(it[1], it[2], it[3], it[4], it[5])
            else:
                self.dma(it[1], it[2], it[3], final=it[4], **it[5])

    @staticmethod
    def interleave(a, b):
        def atoms(lst):
            out, cur, open_pe = [], [], False
            for it in lst:
                cur.append(it)
                if it[0] == "op" and it[1] == "pe":
                    open_pe = not it[5]
                if not open_pe:
                    out.append(cur)
                    cur = []
            if cur:
                out.append(cur)
            return out
        A, B = atoms(a), atoms(b)
        res, i, j = [], 0, 0
        while i < len(A) or j < len(B):
            if j >= len(B) or (i < len(A) and i * len(B) <= j * len(A)):
                res.extend(A[i]); i += 1
            else:
                res.extend(B[j]); j += 1
        return res

    def finish(self):
        for e in ENGS:
            assert self.pending[e] == 0, (e, self.pending[e])
        waits = []
        for t in self.final_dma:
            self._wait_for("sp", t, waits)
        for reg in self.all_dma_regs:
            self._wait_for("sp", ("d", reg, reg.dcnt), waits)
        self.prog["sp"].append(lambda e: [e.wait_ge(s, v) for s, v in waits])
        block = self.es.enter_context(self.nc.Block())
        prog = self.prog

        @block.tensor
        def _(e):
            for f in prog["pe"]:
                f(e)

        @block.scalar
        def _(e):
            for f in prog["act"]:
                f(e)

        @block.vector
        def _(e):
            for f in prog["dve"]:
                f(e)

        @block.gpsimd
        def _(e):
            for f in prog["pool"]:
                f(e)

        @block.sync
        def _(e):
            for f in prog["sp"]:
                f(e)


def unit_cols(w, cols):
    K = w.shape[0] // 128
    return np.ascontiguousarray(w[:, cols].reshape(K, 128, 128).transpose(1, 0, 2))


def swa_q_cols(j, swapped):
    cols = []
    for h in (j, j + 8):
        base = OFF_QA + 64 * h
        idx = np.arange(64)
        if swapped:
            idx = (idx + 32) % 64
        cols.append(base + idx)
    return np.concatenate(cols)


def swa_k_cols(swapped):
    cols = []
    for kv in range(2):
        idx = np.arange(64)
        if swapped:
            idx = (idx + 32) % 64
        cols.append(OFF_KA + 64 * kv + idx)
    return np.concatenate(cols)


def main_units_spec():
    spec = []
    for j in range(8):
        spec.append(("qa%d" % j, ("win", swa_q_cols(j, False))))
        spec.append(("qas%d" % j, ("win", swa_q_cols(j, True))))
    spec.append(("ka", ("win", swa_k_cols(False))))
    spec.append(("kas", ("win", swa_k_cols(True))))
    spec.append(("va", ("win", OFF_VA + np.arange(128))))
    for b in range(8):
        spec.append(("qc%d" % b, ("win", OFF_QC + 128 * b + np.arange(128))))
    for h in range(8):
        spec.append(("qb%d" % h, ("win", OFF_QB + 128 * h + np.arange(128))))
        spec.append(("kb%d" % h, ("win", OFF_KB + 128 * h + np.arange(128))))
        spec.append(("vb%d" % h, ("win", OFF_VB + 128 * h + np.arange(128))))
    for h in range(8):
        spec.append(("zb%d" % h, ("win", OFF_ZB + 128 * h + np.arange(128))))
    for ob in range(8):
        for n in range(3):
            spec.append(("gl%d_%d" % (n, ob), ("win", OFF_GL + 1024 * n + 128 * ob + np.arange(128))))
            spec.append(("br%d_%d" % (n, ob), ("wbr", n, ob)))
    for ob in range(8):
        spec.append(("wo%d" % ob, ("wout", ob)))
    for fb in range(32):
        spec.append(("up%d" % fb, ("wup", fb)))
    for ob in range(8):
        for kq in range(4):
            spec.append(("dn%d_%d" % (ob, kq), ("wdn", ob, kq)))
    return spec


BIAS_NAMES = (["qa%d" % j for j in range(8)] + ["qas%d" % j for j in range(8)] + ["ka", "kas"]
              + ["qc%d" % b for b in range(8)] + ["qb%d" % h for h in range(8)] + ["kb%d" % h for h in range(8)]
              + ["vb%d" % h for h in range(8)] + ["zb%d" % h for h in range(8)]
              + ["gl%d_%d" % (n, ob) for n in range(3) for ob in range(8)])
BIAS_COL = {n: i for i, n in enumerate(BIAS_NAMES)}


def rope_tables(pos):
    half = 32
    inv = ROPE_THETA ** (-np.arange(half, dtype=np.float32) / half)
    ang = pos.astype(np.float32)[None, :] * inv[:, None]
    ang = ang.astype(np.float32)
    cos = np.cos(ang).astype(np.float32)
    sin = np.sin(ang).astype(np.float32)
    cosT = np.concatenate([cos, cos, cos, cos], 0)
    sinT = np.concatenate([-sin, sin, -sin, sin], 0)
    return np.ascontiguousarray(cosT), np.ascontiguousarray(sinT)


def gdn_consts():
    p = np.arange(128)
    blk = p // 64
    loc = p % 64
    j = np.arange(64)
    c = {}
    c["ident"] = np.eye(128, dtype=np.float32)
    same = (blk[:, None] == blk[None, :])
    c["uincl_bd"] = (same & (loc[:, None] <= loc[None, :])).astype(np.float32)
    c["ones_bd"] = same.astype(np.float32)
    c["ustrict_bd"] = (same & (loc[:, None] > loc[None, :])).astype(np.float32)
    c["onesA"] = np.repeat((p < 64)[:, None], 128, 1).astype(np.float32)
    c["onesB"] = np.repeat((p >= 64)[:, None], 128, 1).astype(np.float32)
    c["ust"] = (loc[:, None] <= j[None, :]).astype(np.float32)
    c["nust"] = -c["ust"]
    c["maskLo"] = np.where(j[None, :] <= loc[:, None], 0.0, NEG).astype(np.float32)
    c["maskUp"] = np.where(j[None, :] >= loc[:, None], 0.0, NEG).astype(np.float32)
    c["negstrict"] = np.where(j[None, :] < loc[:, None], -1.0, 0.0).astype(np.float32)
    c["ident_st"] = (j[None, :] == loc[:, None]).astype(np.float32)
    bd = np.zeros((128, 2, 64), np.float32)
    bd[p < 64, 0, :] = 1.0
    bd[p >= 64, 1, :] = 1.0
    c["bdmask"] = bd.reshape(128, 128)
    return c


CST_ORDER = ["ident", "uincl_bd", "ones_bd", "ustrict_bd", "onesA", "onesB", "bdmask", "ust", "nust", "maskLo", "maskUp",
             "negstrict", "ident_st"]


def cst_layout():
    c = gdn_consts()
    off = {}
    o = 0
    for n in CST_ORDER:
        off[n] = (o, c[n].shape[1])
        o += c[n].shape[1]
    arr = np.concatenate([c[n] for n in CST_ORDER], 1)
    return off, np.ascontiguousarray(arr)


def swa_masks():
    q = np.arange(128)
    k = np.arange(256)
    std = np.zeros((128, 256), np.float32)
    a = q < 64
    std[np.ix_(a, k >= 192)] = NEG
    std[np.ix_(~a, k < 64)] = NEG
    samp = np.zeros((128, 256), np.float32)
    samp[np.ix_(a, k >= 192)] = NEG
    samp[np.ix_(~a, (k >= 128) & (k < 192))] = NEG
    return std, samp


def prep_inputs(inp, TP):
    f32 = np.float32
    NMG = TP // NT
    NWG = 7 * NMG
    xp = np.asarray(inp["x_prompt"], f32)[0]
    xs = np.asarray(inp["x_sample"], f32)
    w_in = np.asarray(inp["w_in"], f32)[0]
    b_in = np.asarray(inp["b_in"], f32)[0]
    w_br = np.asarray(inp["w_branch"], f32)[0]
    w_out = np.asarray(inp["w_out"], f32)[0]
    w_up = np.asarray(inp["w_up"], f32)[0]
    w_dn = np.asarray(inp["w_down"], f32)[0]
    w_mem = np.asarray(inp["w_mem_kv"], f32)[0]

    units = []
    for name, kind in main_units_spec():
        if kind[0] == "win":
            units.append(unit_cols(w_in, kind[1]))
        elif kind[0] == "wbr":
            units.append(unit_cols(w_br[kind[1]], 128 * kind[2] + np.arange(128)))
        elif kind[0] == "wout":
            units.append(unit_cols(w_out, 128 * kind[1] + np.arange(128)))
        elif kind[0] == "wup":
            units.append(unit_cols(w_up, 128 * kind[1] + np.arange(128)))
        elif kind[0] == "wdn":
            ob, kq = kind[1], kind[2]
            units.append(unit_cols(w_dn[1024 * kq:1024 * (kq + 1)], 128 * ob + np.arange(128)))
    wmain = np.stack(units, 0).reshape(len(units), 128, 1024)
    wwarm = np.stack([unit_cols(w_in, off + 128 * h + np.arange(128)) for h in range(8) for off in (OFF_KB, OFF_VB)],
                     0).reshape(16, 128, 1024)
    whalo = np.stack([unit_cols(w_in, swa_k_cols(False)), unit_cols(w_in, swa_k_cols(True)),
                      unit_cols(w_in, OFF_VA + np.arange(128))], 0).reshape(3, 128, 1024)
    wmem = np.ascontiguousarray(w_mem.reshape(8, 128, 8, 256).transpose(2, 1, 0, 3)).reshape(8, 128, 2048)
    wab = np.ascontiguousarray(w_in[:, OFF_AB:OFF_AB + 16].reshape(8, 128, 16).transpose(1, 0, 2)).reshape(128, 128)

    bfm = np.zeros((128, len(BIAS_NAMES)), f32)
    spec = dict(main_units_spec())
    for n, i in BIAS_COL.items():
        bfm[:, i] = b_in[spec[n][1]]
    rep = lambda v: np.ascontiguousarray(np.broadcast_to(np.asarray(v, f32)[None, :], (128, len(v))))
    sinks = np.asarray(inp["swa_sinks"], f32)[0]
    conv_w = np.asarray(inp["conv_w"], f32)[0]
    cw = np.ascontiguousarray(conv_w.reshape(4, 24, 128).transpose(2, 1, 0))
    smalls = np.concatenate([
        bfm,
        rep(b_in[OFF_VA:OFF_VA + 128]),
        rep(b_in[OFF_AB:OFF_AB + 16]),
        rep(sinks),
        rep(np.asarray(inp["gdn_a_log"], f32)[0]),
        rep(np.asarray(inp["gdn_dt_bias"], f32)[0]),
        np.asarray(inp["gdn_norm_g"], f32)[0][:, None],
        cw.reshape(128, 96),
    ], 1)
    gains = np.stack([rep(np.asarray(inp[k], f32)[0]) for k in
                      ("g_pre_mix", "g_post_mix", "g_pre_ffn", "g_post_ffn", "g_mem")], 0)
    cst_off, cst = cst_layout()
    mstd, msamp = swa_masks()
    mem_x = np.asarray(inp["mem_prompt"], f32)[0]
    ck = np.asarray(inp["cache_swa_k"], f32)[0].reshape(DEC_BATCH, 128, 128)
    cv = np.asarray(inp["cache_swa_v"], f32)[0].reshape(DEC_BATCH, 128, 128)
    sg = np.asarray(inp["state_gdn"], f32)[0]
    sc = np.asarray(inp["state_conv"], f32)[0]
    cmk = np.asarray(inp["cache_mem_k"], f32)[0].reshape(DEC_BATCH, 256, 1024)
    cmv = np.asarray(inp["cache_mem_v"], f32)[0].reshape(DEC_BATCH, 256, 1024)
    cs_s, sn_s = rope_tables(PAST_LEN + (np.arange(128) % 64))
    maps = []
    for c in range(NCORES):
        xw = np.zeros((7 * TP, 1024), f32)
        if c > 0:
            xw[(7 - c) * TP:] = xp[:c * TP]
        flags = np.zeros((128, NWG + 1), f32)
        flags[:, (7 - c) * NMG:NWG] = 1.0
        flags[:, NWG] = 1.0 if c > 0 else 0.0
        xh = xw[-128:].copy()
        pos_m = c * TP + np.arange(TP)
        cs_m, sn_m = rope_tables(pos_m)
        cs_h, sn_h = rope_tables(c * TP - 128 + np.arange(128))
        mfirst = mstd.copy()
        if c == 0:
            mfirst[:, :128] = NEG
        s0, s1 = 2 * c, 2 * c + 1
        sch = np.ascontiguousarray(sc[s0:s1 + 1].reshape(2, 3, 24, 128).transpose(3, 2, 0, 1))
        m = {
            "xw": xw.reshape(NWG, NT, 1024), "xm": np.ascontiguousarray(xp[c * TP:(c + 1) * TP]).reshape(NMG, NT, 1024),
            "xs": np.ascontiguousarray(xs[s0:s1 + 1]).reshape(128, 1024), "xh": xh, "flags": flags,
            "rope_m": np.ascontiguousarray(np.stack([cs_m, sn_m], 0)), "rope_h": np.stack([cs_h, sn_h], 0),
            "rope_s": np.stack([cs_s, sn_s], 0),
            "masks": np.stack([mstd, mfirst, msamp], 0),
            "cst": cst, "smalls": np.ascontiguousarray(smalls), "gains": gains,
            "wmain": wmain, "wwarm": wwarm, "whalo": whalo, "wmem": wmem, "wab": wab,
            "memx": mem_x,
            "ck": np.ascontiguousarray(ck[s0:s1 + 1]), "cv": np.ascontiguousarray(cv[s0:s1 + 1]),
            "sg": np.ascontiguousarray(sg[s0:s1 + 1].transpose(0, 2, 1, 3)),
            "sch": sch.reshape(128, 24 * 2 * 3),
            "cmk": np.ascontiguousarray(cmk[s0:s1 + 1]), "cmv": np.ascontiguousarray(cmv[s0:s1 + 1]),
        }
        maps.append(m)
    return maps


SM_OFF = {}
_o = 0
for _n, _w in (("bfm", len(BIAS_NAMES)), ("bva", 128), ("bab", 16), ("sinks", 16), ("alog", 8), ("dtb", 8),
               ("gng", 1), ("cw", 96)):
    SM_OFF[_n] = (_o, _w)
    _o += _w
SM_W = _o


class K:
    def __init__(self, TP):
        self.TP = TP
        self.NMG = TP // NT
        self.NWG = 7 * self.NMG
        self.nc = bass.Bass("TRN2", target_bir_lowering=False)
        self.es = ExitStack()
        self.fw = FW(self.nc, self.es)
        self.psn = 0
        self.ps_set = (0, 6)

    def sb(self, name, shape, dt=F32, nreg=1):
        return Buf(self.fw, "s_" + name, shape, dt, "sbuf", nreg)

    def din(self, name, shape):
        return Buf(self.fw, name, shape, F32, "ExternalInput")

    def dout(self, name, shape):
        return Buf(self.fw, name, shape, F32, "ExternalOutput")

    def ps(self, n=1):
        lo, nb = self.ps_set
        if n == 1:
            b = lo + self.psn % nb
            self.psn += 1
            return self.psum.r(b)
        if self.psn % 2:
            self.psn += 1
        b = lo + self.psn % nb
        self.psn += 2
        return self.psum.rr(b, b + 2)

    def ps_long(self, n=1):
        if n == 1:
            return self.psum.r(6)
        return self.psum.rr(6, 8)

    def _sc(self, s, reads):
        if isinstance(s, View):
            reads.append(s)
            return s.ap
        return s

    def tt(self, eng, out, a, b, op):
        self.fw.op(eng, lambda e: e.tensor_tensor(out=out.ap, in0=a.ap, in1=b.ap, op=op), [a, b], [out])

    def ts(self, eng, out, a, s1, op0, s2=None, op1=None):
        reads = [a]
        s1a = self._sc(s1, reads)
        s2a = self._sc(s2, reads)
        if op1 is None:
            self.fw.op(eng, lambda e: e.tensor_scalar(out=out.ap, in0=a.ap, scalar1=s1a, scalar2=None, op0=op0),
                       reads, [out])
        else:
            self.fw.op(eng, lambda e: e.tensor_scalar(out=out.ap, in0=a.ap, scalar1=s1a, scalar2=s2a, op0=op0, op1=op1),
                       reads, [out])

    def stt(self, eng, out, a, s, b, op0, op1):
        reads = [a, b]
        sa = self._sc(s, reads)
        self.fw.op(eng, lambda e: e.scalar_tensor_tensor(out=out.ap, in0=a.ap, scalar=sa, in1=b.ap, op0=op0, op1=op1),
                   reads, [out])

    def act(self, out, a, func, bias=None, scale=None, accum=None):
        reads = [a]
        writes = [out]
        kw = {}
        if bias is not None:
            kw["bias"] = self._sc(bias, reads)
        if scale is not None:
            kw["scale"] = self._sc(scale, reads)
        if accum is not None:
            kw["accum_out"] = accum.ap
            writes.append(accum)
        self.fw.op("act", lambda e: e.activation(out=out.ap, in_=a.ap, func=func, **kw), reads, writes)

    def cp(self, eng, out, a):
        if eng == "act":
            self.act(out, a, AF.Copy)
        else:
            self.fw.op(eng, lambda e: e.tensor_copy(out=out.ap, in_=a.ap), [a], [out])

    def memset(self, out, val):
        self.fw.op("dve", lambda e: e.memset(out.ap, val), [], [out])

    def recip(self, out, a):
        self.fw.op("dve", lambda e: e.reciprocal(out=out.ap, in_=a.ap), [a], [out])

    def rmax(self, out, a):
        self.fw.op("dve", lambda e: e.reduce_max(out=out.ap, in_=a.ap, axis=AX.X), [a], [out])

    def mm(self, out, lhsT, rhs, start=True, stop=True, inc=None):
        if inc is None:
            inc = stop
        self.fw.op("pe", lambda e: e.matmul(out.ap, lhsT=lhsT.ap, rhs=rhs.ap, start=start, stop=stop),
                   [lhsT, rhs], [out], inc=inc)

    def tr(self, out, a, inc=True):
        idn = self.ident
        self.fw.op("pe", lambda e: e.transpose(out.ap, a.ap, idn.ap), [a, idn], [out], inc=inc)

    def dma(self, q, out, a, **kw):
        self.fw.dma(q, out, a, **kw)


class WStream:
    NR = 6
    UPS = 2

    def __init__(self, k):
        self.k = k
        self.ring = k.sb("wring", [128, self.NR, self.UPS, 1024], BF16, nreg=self.NR)
        self.loads = []
        self.issued = 0
        self.cur_load = 0
        self.cur_unit = 0

    def add(self, dbuf, u0, n):
        i = u0
        while i < u0 + n:
            m = min(self.UPS, u0 + n - i)
            self.loads.append((View(dbuf.t[i:i + m].rearrange("u p e -> p u e"), dbuf.regs), m))
            i += m

    def _issue_upto(self, L):
        while self.issued <= min(L, len(self.loads) - 1):
            v, m = self.loads[self.issued]
            slot = self.issued % self.NR
            for u_ in range(m):
                self.k.dma("pool", View(self.ring.t[:, slot, u_, :], [self.ring.regs[slot]]), v[:, u_, :])
            self.issued += 1

    def next(self):
        self._issue_upto(self.cur_load + self.NR - 1)
        v, m = self.loads[self.cur_load]
        slot = self.cur_load % self.NR
        u = View(self.ring.t[:, slot, self.cur_unit, :].rearrange("p (k c) -> p k c", k=8), [self.ring.regs[slot]])
        self.cur_unit += 1
        if self.cur_unit == m:
            self.cur_unit = 0
            self.cur_load += 1
        return u

    def next_load_raw(self):
        assert self.cur_unit == 0
        self._issue_upto(self.cur_load + self.NR - 1)
        slot = self.cur_load % self.NR
        u = View(self.ring.t[:, slot].rearrange("p u e -> p (u e)"), [self.ring.regs[slot]])
        self.cur_load += 1
        return u


class _Stop(Exception):
    pass


def build(TP, debug=None, stop=None, nwarm=None):
    import os as _os
    k = K(TP)
    fw = k.fw
    NMG, NWG = k.NMG, k.NWG
    NU = len(main_units_spec())
    cst_off, _ = cst_layout()
    CW = sum(w for _, w in cst_off.values())

    d_xw = k.din("xw", [NWG, NT, 1024])
    d_xm = k.din("xm", [NMG, NT, 1024])
    d_xs = k.din("xs", [128, 1024])
    d_xh = k.din("xh", [128, 1024])
    d_flags = k.din("flags", [128, NWG + 1])
    d_rope_m = k.din("rope_m", [2, 128, TP])
    d_rope_h = k.din("rope_h", [2, 128, 128])
    d_rope_s = k.din("rope_s", [2, 128, 128])
    d_masks = k.din("masks", [3, 128, 256])
    d_cst = k.din("cst", [128, CW])
    d_smalls = k.din("smalls", [128, SM_W])
    d_gains = k.din("gains", [5, 128, 1024])
    d_wmain = k.din("wmain", [NU, 128, 1024])
    d_wwarm = k.din("wwarm", [16, 128, 1024])
    d_whalo = k.din("whalo", [3, 128, 1024])
    d_wmem = k.din("wmem", [8, 128, 2048])
    d_wab = k.din("wab", [128, 128])
    d_memx = k.din("memx", [256, 1024])
    d_ck = k.din("ck", [2, 128, 128])
    d_cv = k.din("cv", [2, 128, 128])
    d_sg = k.din("sg", [2, 128, 8, 128])
    d_sch = k.din("sch", [128, 144])
    d_cmk = k.din("cmk", [2, 256, 1024])
    d_cmv = k.din("cmv", [2, 256, 1024])

    o_y = k.dout("y", [NMG, NT, 1024])
    o_ys = k.dout("ys", [128, 1024])
    o_pkT = k.dout("pkT", [128, 128])
    o_pv = k.dout("pv", [128, 128])
    o_pS = k.dout("pS", [128, 8, 128])
    o_pcT = k.dout("pcT", [128, 24, 3])
    o_pmk = k.dout("pmk", [256, 1024])
    o_pmv = k.dout("pmv", [256, 1024])
    o_skT = k.dout("skT", [128, 128])
    o_sv = k.dout("sv", [128, 128])
    o_skc = k.dout("skc", [2, 64, 128])
    o_svc = k.dout("svc", [2, 64, 128])
    o_sS = k.dout("sS", [2, 128, 8, 128])
    o_scT = k.dout("scT", [128, 24, 2, 3])
    dbg = {}
    if debug:
        for name, shape in debug.items():
            dbg[name] = k.dout("dbg_" + name, shape)

    k.psum = Buf(fw, "psum", [128, 8, 512], F32, "psum", nreg=8)
    cst = k.sb("cst", [128, CW])
    smalls = k.sb("smalls", [128, SM_W])
    flags = k.sb("flagsb", [128, NWG + 1])
    masks = k.sb("masksb", [128, 3, 256], BF16)
    wab = k.sb("wabb", [128, 8, 16], BF16)
    gb = k.sb("gb", [128, 2, 1024], F32, nreg=2)
    k.gbn = 0
    st = k.sb("st", [128, 8])
    junk = k.sb("junk", [128, 1024], BF16)
    xn = k.sb("xn", [128, 1024])
    oa = xn
    xt = k.sb("xt", [128, 2, 1024], F32, nreg=2)
    hT = k.sb("hT", [128, 8, NT], BF16)
    ones_full = k.sb("ones_full", [128, 128])
    ident_bf = k.sb("ident_bf", [128, 128], BF16)
    nsink = k.sb("nsink", [128, 16])
    negA = k.sb("negA", [128, 8])
    gkv = k.sb("gkv", [128, 16, NT])
    gq = k.sb("gq", [128, 8, NT])
    go = k.sb("go", [128, 8, NT])
    b4a = k.sb("b4a", [128, 8, NT], BF16)
    b4b = k.sb("b4b", [128, 8, NT], BF16)
    oaT = k.sb("oaT", [128, 8, NT], BF16)
    obT = k.sb("obT", [128, 8, NT], BF16)
    ocT = k.sb("ocT", [128, 8, NT], BF16)
    kTg, vTg = gkv[:, 0:8, :], gkv[:, 8:16, :]
    actT = View(gkv.t[:].rearrange("p a c -> p (a c)").bitcast(BF16).rearrange("p (f c) -> p f c", f=32), gkv.regs)
    qTg, moT, oTg, pf = gq, gq, go, go
    mk_tm = View(gq.t[:].rearrange("p a c -> p (a c)").rearrange("p (m e) -> p m e", m=2), gq.regs)
    qTa, mrg, mqT, hfT = b4a, b4a, b4b, b4b
    mkT = k.sb("mkT", [128, 8, 256], BF16)
    mv = k.sb("mv", [128, 2, 1024], BF16)
    ropeb = k.sb("ropeb", [128, 2, NT])
    kTa = k.sb("kTa", [128, 128 + NT], BF16)
    kTf = k.sb("kTf", [128, NT])
    vtm = k.sb("vtm", [128, 3, 128], BF16)
    vtf = k.sb("vtf", [128, 2, 128])
    tmp1 = k.sb("tmp1", [128, NT])
    tmp2 = k.sb("tmp2", [128, NT])
    kTc = k.sb("kTc", [128, 2, 128], BF16)
    vc = k.sb("vc", [128, 2, 128], BF16)
    ckf = k.sb("ckf", [128, 2, 128])
    mx = k.sb("mx", [128, 16])
    negm = k.sb("negm", [128, 16])
    se = k.sb("se", [128, 16])
    esk = k.sb("esk", [128, 16])
    pT = k.sb("pT", [128, 2, 2, 128], BF16, nreg=2)
    k.ptn = 0
    gt3 = k.sb("gt", [128, 3, NT], F32, nreg=3)
    tm3 = k.sb("tm3", [128, NT])
    rl2 = k.sb("rl", [128, 2, NT], BF16, nreg=2)
    S_p = k.sb("S_p", [128, 8, 128])
    S_a = k.sb("S_a", [128, 8, 128])
    hist = k.sb("hist", [128, 24, 3])
    hist_s = k.sb("hist_s", [128, 24, 2, 3])
    halo_hT = k.sb("halo_hT", [128, 8, 3], BF16)
    xpb = k.sb("xpb", [128, 2, NT + 6], F32, nreg=2)
    k.xpn = 0
    cacc2 = k.sb("cacc", [128, 2, NT], F32, nreg=2)
    csil2 = k.sb("csil", [128, 2, NT], F32, nreg=2)
    rnb2 = k.sb("rnb", [128, 2, NT], F32, nreg=2)
    cacc, csil, rnb = cacc2.r(0), csil2.r(0), rnb2.r(0)
    ab = k.sb("ab", [128, 16])
    bet = k.sb("bet", [128, 8])
    la = k.sb("la", [128, 8])
    nla = k.sb("nla", [128, 8])
    gg = k.sb("gg", [128, 32])
    EE = k.sb("EE", [128, 32])
    bkg = k.sb("bkg", [128, 8])
    tq3 = k.sb("tq3", [128, 3])
    g4 = [k.sb("g4_%d" % i, [128, 8, 128]) for i in range(8)]
    g2 = [k.sb("g2_%d" % i, [128, 8, 64]) for i in range(7)]
    N_bd = Tt_bd = g4[0]
    U_bd = vbt = g4[1]
    kbg, kdt, wT, vnew, qdT = g4[2], g4[3], g4[5], g4[6], g4[7]
    ut = g4[4]
    egrow = View(g4[4].t[:].rearrange("p h j -> p (h j)").rearrange("p (c h j) -> p c h j", c=2, h=8), g4[4].regs)
    Dm, DTm, N_st, laU, qkT = g2[0], g2[1], g2[4], g2[5], g2[6]
    t1 = U_st = g2[2]
    nbs = Tt_st = g2[3]

    def bfview(buf, n):
        flat = buf.t[:].rearrange("p h j -> p (h j)").bitcast(BF16)
        return View(flat[:, 0:8 * n].rearrange("p (h j) -> p h j", h=8), buf.regs)
    N_bdb = bfview(g2[0], 128)
    N_stb = bfview(g2[1], 64)
    U_stb = bfview(g2[2], 64)
    Tt_stb = bfview(g2[3], 64)
    U_bdb = bfview(g4[1], 128)
    Tt_bdb = bfview(g4[0], 128)
    vbtb = View(g4[1].t[:].rearrange("p h j -> p (h j)").bitcast(BF16)[:, 1024:2048].rearrange("p (h j) -> p h j", h=8), g4[1].regs)
    kbgb = bfview(g4[2], 128)
    kdtb = bfview(g4[3], 128)
    wTb = bfview(g4[5], 128)
    vnewb = bfview(g4[6], 128)
    qdTb = bfview(g4[7], 128)
    qkTb = bfview(g2[6], 64)
    Sb = bfview(g2[0], 128)

    def C(name):
        o, w = cst_off[name]
        return cst[:, o:o + w]

    def SM(name, a=None, b=None):
        o, w = SM_OFF[name]
        if a is None:
            return smalls[:, o:o + w]
        return smalls[:, o + a:o + b]

    def BIAS(name):
        c = BIAS_COL[name]
        return SM("bfm", c, c + 1)

    k.ident = C("ident")
    k.dma("sp", cst[:], d_cst[:])
    k.dma("sp", smalls[:], d_smalls[:])
    k.dma("sp", flags[:], d_flags[:])
    for m_ in range(3):
        k.dma("pool", masks[:, m_, :], d_masks[m_])
    k.dma("pool", wab[:].re("p k c -> p (k c)"), d_wab[:])
    k.memset(ones_full[:], 1.0)
    k.cp("dve", ident_bf[:], C("ident"))
    k.ts("dve", nsink[:], SM("sinks"), -1.0, ALU.mult)
    k.act(negA[:], SM("alog"), AF.Exp)
    k.ts("dve", negA[:], negA[:], -1.0, ALU.mult)
    k.memset(S_p[:], 0.0)
    k.memset(hist[:], 0.0)

    def gain(i):
        s = k.gbn % 2
        k.gbn += 1
        k.dma("sp", gb.r(s), d_gains[i])
        return gb.r(s)

    ws = WStream(k)
    ws.loads = [(View(d_wmem.t[i].rearrange("p (u e) -> p u e", u=2), d_wmem.regs), 2) for i in range(8)]
    ws.add(d_whalo, 0, 3)
    for g in range(NWG):
        ws.add(d_wwarm, 0, 16)
    for g in range(NMG + 1):
        ws.add(d_wmain, 0, NU)
    if _os.environ.get('DEV_NOPF'):
        ws.loads = ws.loads[:int(_os.environ['DEV_NOPF'])]

    def dump(name, v, bf=False):
        if name in dbg:
            k.dma("pool" if bf else "sp", dbg[name][:], v, final=True)

    def transpose_to(dst, src_tm, t):
        for half in range(2):
            p = k.ps(1)
            for kk in range(4):
                c = (half * 4 + kk) * 128
                k.tr(p[:, kk * 128:(kk + 1) * 128], src_tm[:, c:c + 128], inc=(kk == 3))
            k.cp("act" if half == 0 else "dve", dst[:, half * 4:(half + 1) * 4, t * 128:(t + 1) * 128],
                 p.re("p (k c) -> p k c", k=4))

    def norm_rows(xr, gv, col):
        k.act(junk[:], xr, AF.Square, accum=st[:, col:col + 1])
        k.act(st[:, col + 1:col + 2], st[:, col:col + 1], AF.Sqrt, bias=EPS, scale=1.0 / 1024)
        k.recip(st[:, col + 2:col + 3], st[:, col + 1:col + 2])
        k.stt("dve", xn[:], xr, st[:, col + 2:col + 3], gv, ALU.mult, ALU.mult)

    def load_norm(src_rows, ntile, gain_idx, dst=None):
        gv = gain(gain_idx)
        for t in range(ntile):
            xr = xt.r(t)
            k.dma("sp", xr, src_rows(t))
            norm_rows(xr, gv, 0)
            transpose_to(hT if dst is None else dst, xn, t)

    def proj_fm(u, ncol, src=None):
        s = hT if src is None else src
        p = k.ps(1)
        for kc in range(8):
            k.mm(p[:, 0:ncol], u[:, kc, :], s[:, kc, 0:ncol], start=(kc == 0), stop=(kc == 7))
        return p[:, 0:ncol]

    def build_mkT(tm):
        for blk in range(8):
            p = k.ps(1)
            for mt in range(2):
                k.tr(p[:, mt * 128:(mt + 1) * 128], tm[:, mt, blk * 128:(blk + 1) * 128], inc=(mt == 1))
            k.cp("act" if blk % 2 else "dve", mkT[:, blk, :], p[:, 0:256])

    def chk(tag):
        if stop == tag:
            raise _Stop()

    def _body():
        chk('setup')
        load_norm(lambda t: d_memx[t * 128:(t + 1) * 128, :], 2, 4)
        chk('mem_ln')
        for piece in range(int(_os.environ.get('DEV_PIECES', '8'))):
            wv = ws.next_load_raw().re("p (k c) -> p k c", k=8)
            q4 = piece % 4
            for mt in range(2):
                p = k.ps(1)
                for kc in range(8):
                    k.mm(p[:, 0:256], hT[:, kc, mt * 128:(mt + 1) * 128], wv[:, kc, :], start=(kc == 0), stop=(kc == 7))
                if piece < 4:
                    k.cp("act", mk_tm[:, mt, q4 * 256:(q4 + 1) * 256], p[:, 0:256])
                else:
                    k.cp("act", xt.r(mt)[:, q4 * 256:(q4 + 1) * 256], p[:, 0:256])
                    k.cp("dve", mv[:, mt, q4 * 256:(q4 + 1) * 256], p[:, 0:256])
        chk('mem_mm')
        for mt in range(2):
            k.dma("sp", o_pmk[mt * 128:(mt + 1) * 128, :], mk_tm[:, mt, :], final=True)
            k.dma("sp", o_pmv[mt * 128:(mt + 1) * 128, :], xt.r(mt), final=True)
        build_mkT(mk_tm)
        chk('mem')

        def rope_evac(p, p_sw, bname, bsname, ncol, out_bf, out_f32=None):
            k.stt("dve", tmp1[:, 0:ncol], p, BIAS(bname), ropeb[:, 0, 0:ncol], ALU.add, ALU.mult)
            k.stt("dve", tmp2[:, 0:ncol], p_sw, BIAS(bsname), ropeb[:, 1, 0:ncol], ALU.add, ALU.mult)
            if out_f32 is not None:
                k.tt("dve", out_f32, tmp1[:, 0:ncol], tmp2[:, 0:ncol], ALU.add)
                k.cp("act", out_bf, out_f32)
            else:
                k.tt("dve", out_bf, tmp1[:, 0:ncol], tmp2[:, 0:ncol], ALU.add)

        def v_tm(u, ntile, tile0):
            for t in range(ntile):
                p = k.ps(1)
                for kc in range(8):
                    k.mm(p[:, 0:128], hT[:, kc, t * 128:(t + 1) * 128], u[:, kc, :], start=(kc == 0), stop=(kc == 7))
                k.tt("dve", vtf[:, t, :], p[:, 0:128], SM("bva"), ALU.add)
                k.cp("act", vtm[:, tile0 + t, :], vtf[:, t, :])

        load_norm(lambda t: d_xh[:, :], 1, 0)
        k.dma("sp", ropeb[:, :, 0:128], d_rope_h[:].re("m p c -> p m c"))
        u_k = ws.next()
        pk0 = proj_fm(u_k, 128)
        u_ks = ws.next()
        pks0 = proj_fm(u_ks, 128)
        rope_evac(pk0, pks0, "ka", "kas", 128, kTa[:, 0:128])
        v_tm(ws.next(), 1, 0)
        chk('halo')

        def swa_tile(qcols, keyspecs, pvspecs, mask_v, t_idx):
            for kv in range(2):
                rows = slice(64 * kv, 64 * kv + 64)
                hs = slice(kv * 8, kv * 8 + 8)
                pss = []
                for j in range(8):
                    h = kv * 8 + j
                    if j % 2 == 0:
                        pb = k.ps(1)
                    ps_h = pb[:, (j % 2) * 256:(j % 2 + 1) * 256]
                    pss.append(ps_h)
                    k.mm(ps_h, ident_bf[:], mask_v, start=True, stop=False, inc=False)
                    specs_ = keyspecs(kv)
                    for si_, (orow, ocol, qsub, rhs) in enumerate(specs_):
                        last_ = (si_ == len(specs_) - 1)
                        k.mm(ps_h[orow, ocol], qTa[rows, j, qcols][:, qsub], rhs, start=False, stop=last_, inc=last_)
                    k.rmax(mx[:, h:h + 1], ps_h)
                k.ts("dve", negm[:, hs], mx[:, hs], -0.125, ALU.mult)
                k.tt("dve", negm[:, hs], negm[:, hs], nsink[:, hs], ALU.min)
                for j in range(8):
                    h = kv * 8 + j
                    k.act(pf[:, j, :], pss[j], AF.Exp, bias=negm[:, h:h + 1], scale=0.125, accum=se[:, h:h + 1])
                k.tt("dve", esk[:, hs], negm[:, hs], SM("sinks")[:, hs], ALU.add)
                k.act(esk[:, hs], esk[:, hs], AF.Exp)
                k.tt("dve", se[:, hs], se[:, hs], esk[:, hs], ALU.add)
                k.recip(se[:, hs], se[:, hs])
                po = k.ps_long(1)
                po3 = po.re("p (h d) -> p h d", h=8)
                for j in range(8):
                    s = k.ptn % 2
                    k.ptn += 1
                    pt_ps = k.ps(1)
                    for kt in range(2):
                        k.tr(pt_ps[:, kt * 128:(kt + 1) * 128], pf[:, j, kt * 128:(kt + 1) * 128], inc=(kt == 1))
                    ptv = View(pT.t[:, s], [pT.regs[s]])
                    k.cp("act" if j % 2 else "dve", ptv, pt_ps[:, 0:256].re("p (t q) -> p t q", t=2))
                    groups = pvspecs(kv)
                    for gi_, lst in enumerate(groups):
                        for i, (orow, kt, lsub, rhs) in enumerate(lst):
                            l = ptv[:, kt, :] if lsub is None else ptv[:, kt, lsub]
                            last = (i == len(lst) - 1)
                            k.mm(po3[orow, j, :], l, rhs, start=(i == 0), stop=last,
                                 inc=(last and gi_ == len(groups) - 1))
                k.tt("dve", oa[:, kv * 512:(kv + 1) * 512].re("p (h d) -> p h d", h=8), po3, bcl(se[:, hs], 64), ALU.mult)
            transpose_to(oaT, oa, t_idx)

        def mem_tile(qc, rows, t_idx, finish=True):
            pss = []
            for h in range(4):
                if h % 2 == 0:
                    pb = k.ps(1)
                ps_h = pb[rows, (h % 2) * 256:(h % 2 + 1) * 256]
                pss.append(ps_h)
                for c in range(2):
                    k.mm(ps_h, mqT[:, 2 * h + c, qc][:, rows], mkT[:, 2 * h + c, :], start=(c == 0), stop=(c == 1))
                k.rmax(mx[rows, h:h + 1], ps_h)
            k.ts("dve", negm[rows, 0:4], mx[rows, 0:4], -1.0 / 16, ALU.mult)
            for h in range(4):
                k.act(pf[rows, h, :], pss[h], AF.Exp, bias=negm[rows, h:h + 1], scale=1.0 / 16, accum=se[rows, h:h + 1])
            k.recip(se[rows, 0:4], se[rows, 0:4])
            po2 = k.ps_long(2)
            pov = po2.re("p b (h d) -> p (b h) d", h=2)
            for h in range(4):
                s = k.ptn % 2
                k.ptn += 1
                pt_ps = k.ps(1)
                for kt in range(2):
                    k.tr(pt_ps[:, kt * 128:(kt + 1) * 128], pf[:, h, kt * 128:(kt + 1) * 128], inc=(kt == 1))
                ptv = View(pT.t[:, s], [pT.regs[s]])
                k.cp("act" if h % 2 else "dve", ptv, pt_ps[:, 0:256].re("p (t q) -> p t q", t=2))
                for kt in range(2):
                    k.mm(pov[rows, h, :], ptv[:, kt, rows], mv[:, kt, h * 256:(h + 1) * 256], start=(kt == 0), stop=(kt == 1))
            k.tt("dve", oa[rows].re("p (h d) -> p h d", h=4), pov[rows], bcl(se[rows, 0:4], 256), ALU.mult)
            if finish:
                transpose_to(ocT, oa, t_idx)

        def bc8(v):
            n = v.ap.shape[-1]
            return v.us(1).bc([128, 8, n])

        def bcl(v, n):
            return v.us(2).bc([v.ap.shape[0], v.ap.shape[1], n])

        def gdn_conv(u, bname, blk, ncol, nseq, hist_v, dst, mode, flag=None, prefill=False, hsrc=None):
            L = ncol // nseq
            s = k.xpn % 2
            k.xpn += 1
            cacc, csil, rnb = cacc2.r(s), csil2.r(s), rnb2.r(s)
            xp = View(xpb.t[:, s, 0:nseq * (L + 3)].rearrange("p (s l) -> p s l", s=nseq), [xpb.regs[s]])
            if prefill:
                p3 = k.ps(1)
                for kc in range(8):
                    k.mm(p3[:, 0:3], u[:, kc, :], halo_hT[:, kc, :], start=(kc == 0), stop=(kc == 7))
                k.act(tq3[:], p3[:, 0:3], AF.Identity, bias=BIAS(bname), scale=1.0)
                k.ts("dve", hist_v, tq3[:].us(1), flags[:, NWG:NWG + 1], ALU.mult)
            p = proj_fm(u, ncol, src=hsrc)
            k.cp("dve", xp[:, :, 0:3], hist_v)
            k.act(xp[:, :, 3:3 + L], p.re("p (s l) -> p s l", s=nseq), AF.Identity, bias=BIAS(bname), scale=1.0)
            if flag is None:
                k.cp("dve", hist_v, xp[:, :, L:L + 3])
            else:
                k.ts("dve", hist_v, xp[:, :, L:L + 3], flag, ALU.mult)
            acc = cacc[:, 0:ncol].re("p (s l) -> p s l", s=nseq)
            cwv = lambda i: SM("cw", blk * 4 + i, blk * 4 + i + 1)
            k.ts("dve", acc, xp[:, :, 0:L], cwv(0), ALU.mult)
            for i in range(1, 4):
                k.stt("dve", acc, xp[:, :, i:i + L], cwv(i), acc, ALU.mult, ALU.add)
            if mode == "v":
                k.act(dst, cacc[:, 0:ncol], AF.Silu)
                return
            k.act(csil[:, 0:ncol], cacc[:, 0:ncol], AF.Silu)
            k.tt("dve", cacc[:, 0:ncol], csil[:, 0:ncol], csil[:, 0:ncol], ALU.mult)
            pn = k.ps(1)
            k.mm(pn[:, 0:ncol], ones_full[:], cacc[:, 0:ncol])
            k.act(rnb[:, 0:ncol], pn[:, 0:ncol], AF.Sqrt, bias=EPS, scale=1.0)
            k.recip(rnb[:, 0:ncol], rnb[:, 0:ncol])
            if mode == "k":
                k.tt("dve", dst, csil[:, 0:ncol], rnb[:, 0:ncol], ALU.mult)
            else:
                k.stt("dve", dst, csil[:, 0:ncol], 128.0 ** -0.5, rnb[:, 0:ncol], ALU.mult, ALU.mult)

        def to_bd(dst, src):
            k.tt("dve", (dst if isinstance(dst, View) else dst[:]).re("p h (c j) -> p h c j", c=2),
                 (src if isinstance(src, View) else src[:]).us(2).bc([128, 8, 2, 64]),
                 C("bdmask").re("p (c j) -> p c j", c=2).us(1).bc([128, 8, 2, 64]), ALU.mult)

        def heads8(p2):
            return p2.re("p b (h j) -> p (b h) j", h=4)

        def gdn_tile2(col0, main, Sv, hsrc=None, ksrc=None, vsrc=None):
            hT_ = hT if hsrc is None else hsrc
            kT_ = kTg if ksrc is None else ksrc
            vT_ = vTg if vsrc is None else vsrc
            tc = slice(col0, col0 + 128)
            p = k.ps(1)
            for kc in range(8):
                k.mm(p[:, 0:16], hT_[:, kc, tc], wab[:, kc, :], start=(kc == 0), stop=(kc == 7))
            k.tt("dve", ab[:], p[:, 0:16], SM("bab"), ALU.add)
            k.act(bet[:], ab[:, 8:16], AF.Sigmoid)
            k.tt("dve", la[:], ab[:, 0:8], SM("dtb"), ALU.add)
            k.act(la[:], la[:], AF.Exp)
            k.act(la[:], la[:], AF.Ln, bias=1.0, scale=1.0)
            k.tt("dve", la[:], la[:], negA[:], ALU.mult)
            if main:
                k.ts("dve", nla[:], la[:], -1.0, ALU.mult)
            p = k.ps(1)
            k.mm(p[:, 0:8], C("uincl_bd"), la[:], inc=False)
            k.mm(p[:, 8:16], C("ustrict_bd"), la[:], inc=False)
            k.mm(p[:, 16:24], C("onesA"), la[:], inc=False)
            k.mm(p[:, 24:32], C("onesB"), la[:])
            k.act(EE[:], p[:, 0:32], AF.Exp)
            k.tt("dve", bkg[:], bet[:], EE[:, 0:8], ALU.mult)
            k.tt("dve", laU[:], bc8(C("ust")), bcl(la[:], 64), ALU.mult)
            k.tt("dve", t1[:], bc8(C("nust")), bcl(la[:], 64), ALU.mult)
            p = k.ps(1)
            p3 = p.re("p (h j) -> p h j", h=8)
            k.mm(p3, C("uincl_bd"), bcl(la[:], 64), start=True, stop=False, inc=False)
            k.mm(p3, C("ones_bd"), t1[:], start=False, stop=False, inc=False)
            k.mm(p3, C("ident"), bc8(C("maskLo")), start=False, stop=True)
            k.act(Dm[:], p3, AF.Exp)
            k.tt("dve", nbs[:], bc8(C("negstrict")), bcl(bet[:], 64), ALU.mult)
            k.tt("dve", t1[:], Dm[:], nbs[:], ALU.mult)
            if main:
                p = k.ps(1)
                p3 = p.re("p (h j) -> p h j", h=8)
                k.mm(p3, C("ones_bd"), laU[:], start=True, stop=False, inc=False)
                k.mm(p3, C("uincl_bd"), bcl(nla[:], 64), start=False, stop=False, inc=False)
                k.mm(p3, C("ident"), bc8(C("maskUp")), start=False, stop=True)
                k.act(DTm[:], p3, AF.Exp)
                p2 = k.ps(2)
                k.mm(p2[:, 0].re("p (h j) -> p h j", h=8), C("onesA"), laU[:], inc=False)
                k.mm(p2[:, 1].re("p (h j) -> p h j", h=8), C("onesB"), laU[:])
                k.act(egrow, p2.re("p c (h j) -> p c h j", h=8), AF.Exp)
            pkk3 = k.ps(1).re("p (h j) -> p h j", h=8)
            for h in range(8):
                for c in range(2):
                    r = slice(64 * c, 64 * c + 64)
                    cc = slice(col0 + 64 * c, col0 + 64 * c + 64)
                    k.mm(pkk3[r, h, :], kT_[:, h, cc], kT_[:, h, cc], inc=(h == 7 and c == 1))
            k.tt("dve", N_st[:], pkk3, t1[:], ALU.mult)
            if main:
                pkq3 = k.ps(1).re("p (h j) -> p h j", h=8)
                for h in range(8):
                    for c in range(2):
                        r = slice(64 * c, 64 * c + 64)
                        cc = slice(col0 + 64 * c, col0 + 64 * c + 64)
                        k.mm(pkq3[r, h, :], kT_[:, h, cc], qTg[:, h, cc], inc=(h == 7 and c == 1))
                k.tt("dve", qkTb, pkq3, DTm[:], ALU.mult)
                for c_ in range(2):
                    k.tt("dve", qdTb[:, :, 64 * c_:64 * c_ + 64], qTg[:, :, col0 + 64 * c_:col0 + 64 * c_ + 64],
                         egrow[:, c_], ALU.mult)
            p2v = heads8(k.ps(2))
            for h in range(8):
                k.tr(p2v[:, h, :], kT_[:, h, tc], inc=(h == 7))
            k.tt("dve", kbgb, p2v, bcl(bkg[:], 128), ALU.mult)
            k.tt("dve", kdtb, p2v, bcl(EE[:, 8:16], 128), ALU.mult)
            p2v = heads8(k.ps(2))
            for h in range(8):
                k.tr(p2v[:, h, :], vT_[:, h, tc], inc=(h == 7))
            k.tt("dve", vbtb, p2v, bcl(bet[:], 128), ALU.mult)
            to_bd(N_bd, N_st)
            p2v = heads8(k.ps(2))
            for h in range(8):
                k.tr(p2v[:, h, :], N_bd[:, h, :], inc=(h == 7))
            k.cp("act", U_bdb, p2v)
            k.cp("dve", N_stb, N_st[:])
            to_bd(N_bdb, N_st)
            k.tt("dve", U_stb, U_bdb[:, :, 0:64], U_bdb[:, :, 64:128], ALU.add)
            k.tt("dve", Tt_stb, U_stb, bc8(C("ident_st")), ALU.add)
            for lvl in range(5):
                pn3 = k.ps(1).re("p (h j) -> p h j", h=8)
                for h in range(8):
                    k.mm(pn3[:, h, :], U_bdb[:, h, :], N_stb[:, h, :], inc=(h == 7))
                if lvl < 4:
                    pu3 = k.ps(1).re("p (h j) -> p h j", h=8)
                    for h in range(8):
                        k.mm(pu3[:, h, :], N_bdb[:, h, :], U_stb[:, h, :], inc=(h == 7))
                k.cp("act", N_stb, pn3)
                if lvl < 4:
                    to_bd(U_bdb, pu3)
                    k.cp("act", U_stb, pu3)
                to_bd(N_bdb, N_stb)
                pt3 = k.ps(1).re("p (h j) -> p h j", h=8)
                for h in range(8):
                    k.mm(pt3[:, h, :], N_bdb[:, h, :], Tt_stb[:, h, :], inc=(h == 7))
                k.tt("dve", Tt_stb, Tt_stb, pt3, ALU.add)
            to_bd(Tt_bdb, Tt_stb)
            p2v = heads8(k.ps(2))
            for h in range(8):
                k.mm(p2v[:, h, :], Tt_bdb[:, h, :], vbtb[:, h, :], inc=(h == 7))
            k.cp("act", ut[:], p2v)
            p2v = heads8(k.ps(2))
            for h in range(8):
                k.mm(p2v[:, h, :], kbgb[:, h, :], Tt_bdb[:, h, :], inc=(h == 7))
            k.cp("act", wTb, p2v)
            if main:
                pov = heads8(k.ps_long(2))
            for c in range(2):
                r = slice(64 * c, 64 * c + 64)
                S = Sv[c]
                k.cp("act", Sb, S)
                pvv = heads8(k.ps(2))
                for h in range(8):
                    k.mm(pvv[r, h, :], wTb[:, h, r], Sb[:, h, :], inc=(h == 7))
                k.tt("dve", vnewb[r], ut[r], pvv[r], ALU.subtract)
                if main:
                    for h in range(8):
                        k.mm(pov[:, h, r], Sb[:, h, :], qdTb[:, h, r], start=True, stop=False, inc=False)
                        k.mm(pov[:, h, r], vnewb[r, h, :], qkTb[r, h, :], start=False, stop=True, inc=(h == 7))
                pdv = heads8(k.ps(2))
                for h in range(8):
                    k.mm(pdv[:, h, :], kdtb[r, h, :], vnewb[r, h, :], inc=(h == 7))
                k.tt("dve", S, S, bcl(EE[:, 16 + 8 * c:24 + 8 * c], 128), ALU.mult)
                k.tt("dve", S, S, pdv, ALU.add)
            if main:
                k.cp("act", oTg[:, :, tc], pov)

        hTw = [hT[:], b4a[:]]
        kTw = [kTg, gq[:]]
        vTw = [vTg, go[:]]
        NW_ = NWG if nwarm is None else nwarm

        def warm_B(g):
            par = g % 2
            load_norm(lambda t, g=g: d_xw[g, t * 128:(t + 1) * 128, :], 2, 0, dst=hTw[par])
            fl = flags[:, g:g + 1]
            for h in range(8):
                gdn_conv(ws.next(), "kb%d" % h, 8 + h, NT, 1, hist[:, 8 + h, :].us(1), kTw[par][:, h, :], "k", flag=fl,
                         hsrc=hTw[par])
                gdn_conv(ws.next(), "vb%d" % h, 16 + h, NT, 1, hist[:, 16 + h, :].us(1), vTw[par][:, h, :], "v", flag=fl,
                         hsrc=hTw[par])

        def warm_A(g):
            par = g % 2
            for t in range(2):
                gdn_tile2(t * 128, False, [S_p[:], S_p[:]], hsrc=hTw[par], ksrc=kTw[par], vsrc=vTw[par])
            k.ts("dve", S_p[:], S_p[:], flags[:, g:g + 1], ALU.mult)

        def capture(fn, g, ps_set):
            k.ps_set = ps_set
            fw.capture = []
            fn(g)
            lst = fw.capture
            fw.capture = None
            k.ps_set = (0, 6)
            return lst

        if NW_ > 0:
            fw.replay(capture(warm_B, 0, (4, 2)))
        for g in range(NW_):
            la_ = capture(warm_A, g, (0, 4))
            lb_ = capture(warm_B, g + 1, (4, 2)) if g + 1 < NW_ else []
            fw.replay(FW.interleave(la_, lb_))
            chk('warm1')
        k.cp("dve", halo_hT[:], hTw[(NW_ - 1) % 2][:, :, NT - 3:NT])
        dump("S_warm", S_p[:])
        chk('warm')

        def resid_norm(srcT, ntile, gain_idx, out_dram=None, next_gain=None):
            gv = gain(gain_idx)
            gv2 = gain(next_gain) if next_gain is not None else None
            for t in range(ntile):
                p2f = k.ps(2).re("p b c -> p (b c)")
                for kk in range(8):
                    k.tr(p2f[:, kk * 128:(kk + 1) * 128], srcT[:, kk, t * 128:(t + 1) * 128], inc=(kk == 7))
                norm_rows(p2f, gv, 0)
                xr = xt.r(t)
                k.tt("dve", xr, xr, xn[:], ALU.add)
                if out_dram is not None:
                    k.dma("sp", out_dram(t), xr, final=True)
                if gv2 is not None:
                    norm_rows(xr, gv2, 3)
                    transpose_to(hfT, xn, t)

        def main_group(gi, sample=False):
            ncol = 128 if sample else NT
            ntile = ncol // 128
            nseq = 2 if sample else 1
            A, B = slice(0, 64), slice(64, 128)
            if sample:
                load_norm(lambda t: d_xs[:, :], 1, 0)
                k.dma("sp", ropeb[:, :, 0:128], d_rope_s[:].re("m p c -> p m c"))
            else:
                load_norm(lambda t: d_xm[gi, t * 128:(t + 1) * 128, :], 2, 0)
                k.dma("sp", ropeb[:], d_rope_m[:, :, gi * NT:(gi + 1) * NT].re("m p c -> p m c"))
            for j in range(8):
                pq = proj_fm(ws.next(), ncol)
                pqs = proj_fm(ws.next(), ncol)
                rope_evac(pq, pqs, "qa%d" % j, "qas%d" % j, ncol, qTa[:, j, 0:ncol])
            pk_ = proj_fm(ws.next(), ncol)
            pks_ = proj_fm(ws.next(), ncol)
            rope_evac(pk_, pks_, "ka", "kas", ncol, kTa[:, 128:128 + ncol], out_f32=kTf[:, 0:ncol])
            v_tm(ws.next(), ntile, 1)
            if gi == 0 and not sample:
                dump("qTa", qTa[:].re("p k c -> p (k c)"), True)
                dump("kTa", kTa[:, 0:256], True)
            for b in range(8):
                pm = proj_fm(ws.next(), ncol)
                k.act(mqT[:, b, 0:ncol], pm, AF.Identity, bias=BIAS("qc%d" % b), scale=1.0)
            if sample:
                k.dma("sp", hist_s[:].re("p b s i -> p (b s i)"), d_sch[:])
                k.dma("sp", S_a[:], d_sg[0])
                k.dma("sp", S_p[:], d_sg[1])
            def stream_attn():
                if sample:
                    for s in range(2):
                        k.dma("sp", ckf[:, s, :], d_ck[s])
                        k.dma("pool", vc[:, s, :], d_cv[s])
                        p = k.ps(1)
                        k.tr(p[:, 0:128], ckf[:, s, :])
                        k.cp("dve", kTc[:, s, :], p[:, 0:128])

                    def keyspecs(kv):
                        r = slice(64 * kv, 64 * kv + 64)
                        return [(A, slice(0, 128), A, kTc[r, 0, :]), (B, slice(0, 128), B, kTc[r, 1, :]),
                                (slice(0, 128), slice(128, 256), slice(0, 128), kTa[r, 128:256])]

                    def pvspecs(kv):
                        c = slice(64 * kv, 64 * kv + 64)
                        return [[(A, 0, A, vc[:, 0, c]), (A, 1, A, vtm[:, 1, c])],
                                [(B, 0, B, vc[:, 1, c]), (B, 1, B, vtm[:, 1, c])]]
                    swa_tile(slice(0, 128), keyspecs, pvspecs, masks[:, 2, :], 0)
                else:
                    for t in range(ntile):
                        def keyspecs(kv, t=t):
                            r = slice(64 * kv, 64 * kv + 64)
                            return [(slice(0, 128), slice(0, 256), slice(0, 128), kTa[r, t * 128:t * 128 + 256])]

                        def pvspecs(kv, t=t):
                            c = slice(64 * kv, 64 * kv + 64)
                            return [[(slice(0, 128), 0, None, vtm[:, t, c]), (slice(0, 128), 1, None, vtm[:, t + 1, c])]]
                        mv_ = masks[:, 1, :] if (gi == 0 and t == 0) else masks[:, 0, :]
                        swa_tile(slice(t * 128, (t + 1) * 128), keyspecs, pvspecs, mv_, t)
                if sample:
                    k.dma("sp", o_skT[:], kTf[:, 0:128], final=True)
                    k.dma("sp", o_sv[:], vtf[:, 0, :], final=True)
                    for s in range(2):
                        k.dma("sp", o_skc[s], d_ck[s, 64:128, :], final=True)
                        k.dma("sp", o_svc[s], d_cv[s, 64:128, :], final=True)
                else:
                    if gi == NMG - 1:
                        k.dma("sp", o_pkT[:], kTf[:, NT - 128:NT], final=True)
                        k.dma("sp", o_pv[:], vtf[:, 1, :], final=True)
                    k.cp("dve", kTa[:, 0:128], kTa[:, NT:NT + 128])
                    k.cp("act", vtm[:, 0, :], vtm[:, 2, :])
                if gi == 0 and not sample:
                    dump("oaT", oaT[:].re("p k c -> p (k c)"), True)
                if sample:
                    for s in range(2):
                        rows = A if s == 0 else B
                        for mt in range(2):
                            k.dma("sp", mk_tm[:, mt, :], d_cmk[s, mt * 128:(mt + 1) * 128, :])
                            k.dma("pool", mv[:, mt, :], d_cmv[s, mt * 128:(mt + 1) * 128, :])
                        build_mkT(mk_tm)
                        mem_tile(slice(0, 128), rows, 0, finish=(s == 1))
                else:
                    for t in range(ntile):
                        mem_tile(slice(t * 128, (t + 1) * 128), slice(0, 128), t)
                if gi == 0 and not sample:
                    dump("ocT", ocT[:].re("p k c -> p (k c)"), True)

            def stream_conv():
                for h in range(8):
                    hv = (lambda b: hist_s[:, b, :, :]) if sample else (lambda b: hist[:, b, :].us(1))
                    gdn_conv(ws.next(), "qb%d" % h, h, ncol, nseq, hv(h), qTg[:, h, 0:ncol], "q",
                             prefill=(gi == 0 and not sample))
                    gdn_conv(ws.next(), "kb%d" % h, 8 + h, ncol, nseq, hv(8 + h), kTg[:, h, 0:ncol], "k")
                    gdn_conv(ws.next(), "vb%d" % h, 16 + h, ncol, nseq, hv(16 + h), vTg[:, h, 0:ncol], "v")

            if sample or stop is not None:
                stream_attn()
                stream_conv()
            else:
                k.ps_set = (0, 5)
                fw.capture = []
                stream_attn()
                ly_ = fw.capture
                k.ps_set = (5, 1)
                fw.capture = []
                stream_conv()
                lx_ = fw.capture
                fw.capture = None
                k.ps_set = (0, 6)
                fw.replay(FW.interleave(ly_, lx_))
            for t in range(ntile):
                gdn_tile2(t * 128, True, [S_a[:], S_p[:]] if sample else [S_p[:], S_p[:]])
            if sample:
                k.dma("sp", o_sS[0], S_a[:], final=True)
                k.dma("sp", o_sS[1], S_p[:], final=True)
                k.dma("sp", o_scT[:], hist_s[:], final=True)
            elif gi == NMG - 1:
                k.dma("sp", o_pS[:], S_p[:], final=True)
                k.dma("sp", o_pcT[:], hist[:], final=True)
            if gi == 0 and not sample:
                dump("oTg", oTg[:].re("p k c -> p (k c)"))
            dump("qTg", qTg[:].re("p k c -> p (k c)"))
            chk('gdn')
            for h in range(8):
                k.tt("dve", cacc[:, 0:ncol], oTg[:, h, 0:ncol], oTg[:, h, 0:ncol], ALU.mult)
                pn = k.ps(1)
                k.mm(pn[:, 0:ncol], ones_full[:], cacc[:, 0:ncol])
                k.act(rnb[:, 0:ncol], pn[:, 0:ncol], AF.Sqrt, bias=EPS, scale=1.0 / 128)
                k.recip(rnb[:, 0:ncol], rnb[:, 0:ncol])
                k.stt("dve", cacc[:, 0:ncol], oTg[:, h, 0:ncol], SM("gng"), rnb[:, 0:ncol], ALU.mult, ALU.mult)
                pz = proj_fm(ws.next(), ncol)
                k.act(csil[:, 0:ncol], pz, AF.Silu, bias=BIAS("zb%d" % h), scale=1.0)
                k.tt("dve", obT[:, h, 0:ncol], cacc[:, 0:ncol], csil[:, 0:ncol], ALU.mult)
            if gi == 0 and not sample:
                dump("obT", obT[:].re("p k c -> p (k c)"), True)
            chk('gdnout')
            srcs = [oaT, obT, ocT]
            for ob in range(8):
                for n in range(3):
                    pg = proj_fm(ws.next(), ncol)
                    gt = gt3.r(n)
                    k.act(gt[:, 0:ncol], pg, AF.Sigmoid, bias=BIAS("gl%d_%d" % (n, ob)), scale=1.0)
                    pb_ = proj_fm(ws.next(), ncol, src=srcs[n])
                    tn_ = (tmp1, tmp2, tm3)[n]
                    k.tt("dve", tn_[:, 0:ncol], pb_, gt[:, 0:ncol], ALU.mult)
                k.tt("dve", tmp1[:, 0:ncol], tmp1[:, 0:ncol], tmp2[:, 0:ncol], ALU.add)
                k.tt("dve", mrg[:, ob, 0:ncol], tmp1[:, 0:ncol], tm3[:, 0:ncol], ALU.add)
            for ob in range(8):
                po_ = proj_fm(ws.next(), ncol, src=mrg)
                k.cp("act", moT[:, ob, 0:ncol], po_)
            if gi == 0 and not sample:
                dump("mrg", mrg[:].re("p k c -> p (k c)"), True)
                dump("moT", moT[:].re("p k c -> p (k c)"))
            resid_norm(moT, ntile, 1, next_gain=2)
            if gi == 0 and not sample:
                dump("x1", xt[:].re("p k c -> p (k c)"))
                dump("hfT", hfT[:].re("p k c -> p (k c)"), True)
            chk('merge')
            for fb in range(32):
                pu_ = proj_fm(ws.next(), ncol, src=hfT)
                rl = rl2.r(fb % 2)
                k.act(rl[:, 0:ncol], pu_, AF.Relu)
                k.tt("dve", actT[:, fb, 0:ncol], rl[:, 0:ncol], rl[:, 0:ncol], ALU.mult)
            for ob in range(8):
                p = k.ps(1)
                for kq in range(4):
                    u = ws.next()
                    for kc in range(8):
                        k.mm(p[:, 0:ncol], u[:, kc, :], actT[:, kq * 8 + kc, 0:ncol],
                             start=(kq == 0 and kc == 0), stop=(kq == 3 and kc == 7))
                k.cp("act", moT[:, ob, 0:ncol], p[:, 0:ncol])
            if sample:
                resid_norm(moT, 1, 3, out_dram=lambda t: o_ys[:, :])
            else:
                resid_norm(moT, 2, 3, out_dram=lambda t: o_y[gi, t * 128:(t + 1) * 128, :])

        for g in range(NMG):
            main_group(g)
            chk('main%d' % g)
        main_group(0, sample=True)

    try:
        _body()
    except _Stop:
        pass
    fw.finish()
    return k


def run(inp, TP, debug=None):
    maps = prep_inputs(inp, TP)
    kb = build(TP, debug)
    res = run_bass_kernel_spmd(kb.nc, maps, core_ids=list(range(NCORES)))
    return res.results, kb


def assemble(r, TP):
    y = np.concatenate([r[c]["y"].reshape(TP, 1024) for c in range(NCORES)], 0)[None]
    ys = np.concatenate([r[c]["ys"].reshape(2, 64, 1024) for c in range(NCORES)], 0)
    L = NCORES - 1
    p_k = np.ascontiguousarray(r[L]["pkT"].T).reshape(1, 1, 128, 2, 64)
    p_v = r[L]["pv"].reshape(1, 1, 128, 2, 64)
    p_S = np.ascontiguousarray(r[L]["pS"].transpose(1, 0, 2)).reshape(1, 1, 8, 128, 128)
    p_c = np.ascontiguousarray(r[L]["pcT"].transpose(2, 1, 0)).reshape(1, 1, 3, 3072)
    p_mk = r[0]["pmk"].reshape(1, 1, 256, 4, 256)
    p_mv = r[0]["pmv"].reshape(1, 1, 256, 4, 256)
    sk, sv, sS, sc = [], [], [], []
    for c in range(NCORES):
        kn = np.ascontiguousarray(r[c]["skT"].T).reshape(2, 64, 128)
        vn = r[c]["sv"].reshape(2, 64, 128)
        sk.append(np.concatenate([r[c]["skc"], kn], 1))
        sv.append(np.concatenate([r[c]["svc"], vn], 1))
        sS.append(r[c]["sS"].transpose(0, 2, 1, 3))
        sc.append(r[c]["scT"].transpose(2, 3, 1, 0).reshape(2, 3, 3072))
    s_k = np.concatenate(sk, 0).reshape(1, 16, 128, 2, 64)
    s_v = np.concatenate(sv, 0).reshape(1, 16, 128, 2, 64)
    s_S = np.ascontiguousarray(np.concatenate(sS, 0)).reshape(1, 16, 8, 128, 128)
    s_c = np.ascontiguousarray(np.concatenate(sc, 0)).reshape(1, 16, 3, 3072)
    outs = (y, ys, p_k, p_v, p_S, p_c, p_mk, p_mv, s_k, s_v, s_S, s_c)
    return tuple(np.ascontiguousarray(o, dtype=np.float32) for o in outs)


def kernel(**inputs):
    TP = SEQ // NCORES
    r, _ = run(inputs, TP)
    return assemble(r, TP)
```
